# Optimizing a Trainium2 kernel written in Bass

```python
import jax, jax.numpy as jnp
from jax import lax
import numpy as np

D_MODEL = 2048
BATCH = 2
SEQ = 4096
DEPTH = 4
DEC_BATCH = 32
DEC_SEQ = 16
PAST_LEN = 4096

CHUNK = 64
N_MIXERS = 4
EPS = 1e-6
POOL_WINDOWS = (2, 4, 8, 16)
N_POOL_GROUPS = len(POOL_WINDOWS)
POOL_GROUP = D_MODEL // N_POOL_GROUPS
POOL_HIST = max(POOL_WINDOWS) - 1
GMLP_CHUNK = 128
GMLP_HEADS = 8
GMLP_WIDTH = D_MODEL
GMLP_HEAD_DIM = GMLP_WIDTH // GMLP_HEADS
SCONV_WIDTH = 3
SCONV_HIST = SCONV_WIDTH - 1
CCONV_WIDTH = 31
CCONV_HIST = CCONV_WIDTH - 1
D_FF = -(-8 * D_MODEL // (3 * 256)) * 256

kernel_name = "streaming_pool_gmlp_shortconv_conformer_trunk"


def rmsnorm(x, g):
    xf = x.astype(jnp.float32)
    y = xf * lax.rsqrt(jnp.mean(xf * xf, axis=-1, keepdims=True) + EPS)
    return (y * g.astype(jnp.float32)).astype(x.dtype)


def layernorm(x, g, b):
    xf = x.astype(jnp.float32)
    mu = jnp.mean(xf, axis=-1, keepdims=True)
    xc = xf - mu
    var = jnp.mean(xc * xc, axis=-1, keepdims=True)
    y = xc * lax.rsqrt(var + EPS) * g.astype(jnp.float32) + b.astype(jnp.float32)
    return y.astype(x.dtype)


def causal_depthwise_conv(x, hist, w):
    K, C = w.shape
    xe = jnp.concatenate([hist.astype(x.dtype), x], axis=1)
    y = lax.conv_general_dilated(xe, w.astype(x.dtype)[:, None, :], window_strides=(1,), padding='VALID',
                                 dimension_numbers=('NWC', 'WIO', 'NWC'), feature_group_count=C)
    return y, xe[:, -(K - 1):, :]


def pool_mixer(h, hist, pos0, pool_w, pool_scale):
    B, S, _ = h.shape
    xe = jnp.concatenate([hist.astype(h.dtype), h], axis=1)
    xf = xe.astype(jnp.float32)
    cs = jnp.concatenate([jnp.zeros_like(xf[:, :1]), jnp.cumsum(xf, axis=1)], axis=1)
    end = cs[:, POOL_HIST + 1:]
    pos = pos0 + jnp.arange(S)
    pooled = []
    for g, w in enumerate(POOL_WINDOWS):
        sl = slice(g * POOL_GROUP, (g + 1) * POOL_GROUP)
        start = cs[:, POOL_HIST + 1 - w: POOL_HIST + 1 - w + S, sl]
        cnt = jnp.minimum(pos + 1, w).astype(jnp.float32)[None, :, None]
        pooled.append((end[..., sl] - start) / cnt)
    pooled = jnp.concatenate(pooled, axis=-1)
    diff = (pooled - h.astype(jnp.float32)).astype(h.dtype).reshape(B, S, N_POOL_GROUPS, POOL_GROUP)
    y = jnp.einsum('bsgc,gcd->bsgd', diff, pool_w).reshape(B, S, D_MODEL)
    return y * pool_scale, xe[:, -POOL_HIST:, :]


def gmlp_mixer(h, w_in, b_in, ln_g, ln_b, w_s, b_s, w_out):
    B, S, _ = h.shape
    z = jax.nn.gelu(h @ w_in + b_in)
    u, v = jnp.split(z, 2, axis=-1)
    v = layernorm(v, ln_g, ln_b)
    n_chunks = -(-S // GMLP_CHUNK)
    pad = n_chunks * GMLP_CHUNK - S
    idx = jnp.arange(GMLP_CHUNK)
    mask = (idx[None, :] // CHUNK) <= (idx[:, None] // CHUNK)
    ws = jnp.where(mask[None], w_s, jnp.zeros((), w_s.dtype))
    vp = jnp.pad(v, ((0, 0), (0, pad), (0, 0))).reshape(B, n_chunks, GMLP_CHUNK, GMLP_HEADS, GMLP_HEAD_DIM)
    s = jnp.einsum('hij,bcjhd->bcihd', ws, vp) + b_s.T[None, None, :, :, None]
    s = s.reshape(B, n_chunks * GMLP_CHUNK, GMLP_WIDTH)[:, :S]
    return (u * s) @ w_out, v


def short_conv_mixer(h, hist, w_in, conv_w, w_out):
    b_gate, c_gate, xin = jnp.split(h @ w_in, 3, axis=-1)
    conv, new_hist = causal_depthwise_conv(c_gate * xin, hist, conv_w)
    return (b_gate * conv) @ w_out, new_hist


def conformer_conv_mixer(h, hist, w_pw1, b_pw1, dw_w, dw_b, ln_g, ln_b, w_pw2, b_pw2):
    a, g = jnp.split(h @ w_pw1 + b_pw1, 2, axis=-1)
    glu = a * jax.nn.sigmoid(g)
    conv, new_hist = causal_depthwise_conv(glu, hist, dw_w)
    z = jax.nn.silu(layernorm(conv + dw_b, ln_g, ln_b))
    return z @ w_pw2 + b_pw2, new_hist


def swiglu(h, w_gate, w_up, w_down):
    return (jax.nn.silu(h @ w_gate) * (h @ w_up)) @ w_down


def trunk(x, pool_hist, sconv_hist, cconv_hist, pos0, p):
    gmlp_v = None
    for i in range(DEPTH):
        h = rmsnorm(x, p['norm_mix_g'][i])
        m = i % N_MIXERS
        if m == 0:
            y, pool_hist = pool_mixer(h, pool_hist, pos0, p['pool_w'], p['pool_scale'])
        elif m == 1:
            y, gmlp_v = gmlp_mixer(h, p['gmlp_w_in'], p['gmlp_b_in'], p['gmlp_ln_g'], p['gmlp_ln_b'],
                                   p['gmlp_w_s'], p['gmlp_b_s'], p['gmlp_w_out'])
        elif m == 2:
            y, sconv_hist = short_conv_mixer(h, sconv_hist, p['sconv_w_in'], p['sconv_conv_w'], p['sconv_w_out'])
        else:
            y, cconv_hist = conformer_conv_mixer(h, cconv_hist, p['cconv_w_pw1'], p['cconv_b_pw1'],
                                                 p['cconv_dw_w'], p['cconv_dw_b'], p['cconv_ln_g'],
                                                 p['cconv_ln_b'], p['cconv_w_pw2'], p['cconv_b_pw2'])
        x = x + y
        h = rmsnorm(x, p['norm_ffn_g'][i])
        x = x + swiglu(h, p['ffn_w_gate'][i], p['ffn_w_up'][i], p['ffn_w_down'][i])
    return rmsnorm(x, p['norm_final_g']), pool_hist, gmlp_v, sconv_hist, cconv_hist


def setup_inputs(seed: int = 0) -> dict:
    key = jax.random.key(seed)
    ks = jax.random.split(key, 40)
    f32 = jnp.float32

    def nrm(k, shape, fan_in):
        return jax.random.normal(k, shape, f32) * (fan_in ** -0.5)

    def gain(k, shape):
        return 1.0 + 0.02 * jax.random.normal(k, shape, f32)

    def bias(k, shape):
        return 0.02 * jax.random.normal(k, shape, f32)

    D = D_MODEL
    return {
        'x_prompt': jax.random.normal(ks[0], (BATCH, SEQ, D), f32),
        'x_sample': jax.random.normal(ks[1], (DEC_BATCH, DEC_SEQ, D), f32),
        'state_pool': jax.random.normal(ks[2], (DEC_BATCH, POOL_HIST, D), f32),
        'state_sconv': jax.random.normal(ks[3], (DEC_BATCH, SCONV_HIST, D), f32),
        'state_cconv': 0.5 * jax.random.normal(ks[4], (DEC_BATCH, CCONV_HIST, D), f32),
        'norm_mix_g': gain(ks[5], (DEPTH, D)),
        'norm_ffn_g': gain(ks[6], (DEPTH, D)),
        'norm_final_g': gain(ks[7], (D,)),
        'pool_w': nrm(ks[8], (N_POOL_GROUPS, POOL_GROUP, POOL_GROUP), POOL_GROUP),
        'pool_scale': gain(ks[9], (D,)),
        'gmlp_w_in': nrm(ks[10], (D, 2 * GMLP_WIDTH), D),
        'gmlp_b_in': bias(ks[11], (2 * GMLP_WIDTH,)),
        'gmlp_ln_g': gain(ks[12], (GMLP_WIDTH,)),
        'gmlp_ln_b': bias(ks[13], (GMLP_WIDTH,)),
        'gmlp_w_s': nrm(ks[14], (GMLP_HEADS, GMLP_CHUNK, GMLP_CHUNK), GMLP_CHUNK),
        'gmlp_b_s': gain(ks[15], (GMLP_HEADS, GMLP_CHUNK)),
        'gmlp_w_out': nrm(ks[16], (GMLP_WIDTH, D), GMLP_WIDTH),
        'sconv_w_in': nrm(ks[17], (D, 3 * D), D),
        'sconv_conv_w': nrm(ks[18], (SCONV_WIDTH, D), SCONV_WIDTH),
        'sconv_w_out': nrm(ks[19], (D, D), D),
        'cconv_w_pw1': nrm(ks[20], (D, 2 * D), D),
        'cconv_b_pw1': bias(ks[21], (2 * D,)),
        'cconv_dw_w': nrm(ks[22], (CCONV_WIDTH, D), CCONV_WIDTH),
        'cconv_dw_b': bias(ks[23], (D,)),
        'cconv_ln_g': gain(ks[24], (D,)),
        'cconv_ln_b': bias(ks[25], (D,)),
        'cconv_w_pw2': nrm(ks[26], (D, D), D),
        'cconv_b_pw2': bias(ks[27], (D,)),
        'ffn_w_gate': nrm(ks[28], (DEPTH, D, D_FF), D),
        'ffn_w_up': nrm(ks[29], (DEPTH, D, D_FF), D),
        'ffn_w_down': nrm(ks[30], (DEPTH, D_FF, D), D_FF),
    }


def reference(x_prompt, x_sample, state_pool, state_sconv, state_cconv,
              norm_mix_g, norm_ffn_g, norm_final_g,
              pool_w, pool_scale,
              gmlp_w_in, gmlp_b_in, gmlp_ln_g, gmlp_ln_b, gmlp_w_s, gmlp_b_s, gmlp_w_out,
              sconv_w_in, sconv_conv_w, sconv_w_out,
              cconv_w_pw1, cconv_b_pw1, cconv_dw_w, cconv_dw_b, cconv_ln_g, cconv_ln_b, cconv_w_pw2, cconv_b_pw2,
              ffn_w_gate, ffn_w_up, ffn_w_down):
    p = dict(norm_mix_g=norm_mix_g, norm_ffn_g=norm_ffn_g, norm_final_g=norm_final_g,
             pool_w=pool_w, pool_scale=pool_scale,
             gmlp_w_in=gmlp_w_in, gmlp_b_in=gmlp_b_in, gmlp_ln_g=gmlp_ln_g, gmlp_ln_b=gmlp_ln_b,
             gmlp_w_s=gmlp_w_s, gmlp_b_s=gmlp_b_s, gmlp_w_out=gmlp_w_out,
             sconv_w_in=sconv_w_in, sconv_conv_w=sconv_conv_w, sconv_w_out=sconv_w_out,
             cconv_w_pw1=cconv_w_pw1, cconv_b_pw1=cconv_b_pw1, cconv_dw_w=cconv_dw_w, cconv_dw_b=cconv_dw_b,
             cconv_ln_g=cconv_ln_g, cconv_ln_b=cconv_ln_b, cconv_w_pw2=cconv_w_pw2, cconv_b_pw2=cconv_b_pw2,
             ffn_w_gate=ffn_w_gate, ffn_w_up=ffn_w_up, ffn_w_down=ffn_w_down)
    bp = x_prompt.shape[0]
    dt = x_prompt.dtype
    y_prompt, pool_p, _, sconv_p, cconv_p = trunk(
        x_prompt,
        jnp.zeros((bp, POOL_HIST, D_MODEL), dt),
        jnp.zeros((bp, SCONV_HIST, D_MODEL), dt),
        jnp.zeros((bp, CCONV_HIST, D_MODEL), dt),
        0, p)
    y_sample, pool_s, gmlp_v_s, sconv_s, cconv_s = trunk(
        x_sample, state_pool, state_sconv, state_cconv, PAST_LEN, p)
    return (y_prompt, y_sample, pool_p, pool_s, gmlp_v_s, sconv_p, sconv_s, cconv_p, cconv_s)
```

```python
import numpy as np
import concourse.bass as bass
import concourse.mybir as mybir
from concourse.bass_utils import run_bass_kernel_spmd

F32 = mybir.dt.float32
BF16 = mybir.dt.bfloat16
F32R = mybir.dt.float32r
AF = mybir.ActivationFunctionType
ALU = mybir.AluOpType

D = 2048
KC = 16
DFF = 5632
NCORE = 8
DEPTH = 4
EPS = 1e-6
HALO = 128
PRE = 16
TP = PRE + HALO + 1024
T = TP + 64
C0 = PRE
MAIN0 = PRE + HALO
TILES = [(16, 528), (528, 1040), (1040, 1232)]
NTILES = [(0, 512), (512, 1024), (1024, 1232)]
GROUPS = [(16, 400), (400, 784), (784, 1168), (1168, 1232)]
TILES_H = [(112, 624), (624, 1136), (1136, 1232)]
TILES_M = [(144, 656), (656, 1168), (1168, 1232)]
GROUPS3 = [(114, 498), (498, 882), (882, 1168), (1168, 1232)]
BOUNDS = sorted(set([0, 16, 400, 528, 784, 1040, 1168, 1232, 112, 624, 1136, 144, 656, 114, 498, 882]))
NSLOT = 4
SLOT_ELEMS = 4096
SCR_WORDS = 11520

VEC_NAMES = (["gmix%d" % i for i in range(4)] + ["gffn%d" % i for i in range(4)] +
             ["gfin", "pscale", "bu", "bv", "lng", "lnb", "scw0", "scw1", "scw2", "ba", "bg"] +
             ["dww%d" % i for i in range(31)] + ["dwb", "clg", "clb", "b2"])
VOFF = {n: i * 16 for i, n in enumerate(VEC_NAMES)}
NV = len(VEC_NAMES) * 16


def bk(c0, c1):
    return [i for i in range(len(BOUNDS) - 1) if BOUNDS[i] < c1 and BOUNDS[i + 1] > c0]


class Op:
    __slots__ = ("eng", "fn", "deps", "sig", "dma_ch", "dma_cnt", "name", "small")

    def __init__(self, eng, fn, name=""):
        self.eng = eng
        self.fn = fn
        self.deps = []
        self.sig = None
        self.dma_ch = None
        self.dma_cnt = 0
        self.name = name
        self.small = False


class Sched:
    ENGS = ["pe", "act", "dve", "pool", "sp"]

    def __init__(self):
        self.ops = {e: [] for e in self.ENGS}
        self.lastw = {}
        self.readers = {}
        self.dma_cnt = {}
        self.dma_last = {}
        self.nch = 0
        self.out_dmas = []

    def new_channel(self):
        self.nch += 1
        return self.nch - 1

    def add(self, eng, fn, reads=(), writes=(), strict=(), dma_ch=None, name="", small=False):
        op = Op(eng, fn, name)
        op.small = small
        deps = {}

        def dep(o, st):
            if o is None or o is op:
                return
            if o.dma_ch is None and o.eng == eng and not st and not (o.small and eng != "pe"):
                return
            deps[id(o)] = o

        for k in reads:
            dep(self.lastw.get(k), isinstance(k, tuple) and k[0] == "ps" and eng != "pe")
        for k in strict:
            dep(self.lastw.get(k), True)
        for k in writes:
            dep(self.lastw.get(k), False)
            for r in self.readers.get(k, ()):
                dep(r, False)
        if dma_ch is not None:
            op.dma_ch = dma_ch
            self.dma_cnt[dma_ch] = self.dma_cnt.get(dma_ch, 0) + 1
            op.dma_cnt = self.dma_cnt[dma_ch]
            prev = self.dma_last.get(dma_ch)
            if prev is not None:
                deps[id(prev)] = prev
            self.dma_last[dma_ch] = op
        for k in list(reads) + list(strict):
            self.readers.setdefault(k, []).append(op)
        for k in writes:
            self.lastw[k] = op
            self.readers[k] = []
        op.deps = list(deps.values())
        self.ops[eng].append(op)
        return op

    def barrier(self, extra=()):
        def fn(e):
            return e.nop()
        op = Op("dve", None, "barrier")
        deps = {}
        for e in self.ENGS:
            if self.ops[e]:
                o = self.ops[e][-1]
                if e != "dve" or o.dma_ch is not None:
                    deps[id(o)] = o
        for ch, o in self.dma_last.items():
            deps[id(o)] = o
        op.deps = list(deps.values())
        self.ops["dve"].append(op)
        self.lastw["__bar"] = op
        self.readers["__bar"] = []
        self.bar = op
        return op


def build_program(debug_stop=None):
    nc = bass.Bass("TRN2", target_bir_lowering=False)
    S = Sched()

    def din(name, shape):
        return nc.dram_tensor(name, list(shape), F32, kind="ExternalInput").ap()

    def dout(name, shape):
        return nc.dram_tensor(name, list(shape), F32, kind="ExternalOutput").ap()

    xin = din("xin", [T, D])
    st_pool = din("st_pool", [60, D])
    st_sconv = din("st_sconv", [8, D])
    st_cconv = din("st_cconv", [120, D])
    vecs_d = din("vecs", [128, NV])
    meta_d = din("meta", [128, 65])
    ident_d = din("ident", [128, 128])
    pool_w = din("pool_w", [4, 512, 512])
    gmlp_w_in = din("gmlp_w_in", [D, 2 * D])
    gmlp_w_s = din("gmlp_w_s", [8, 128, 128])
    gmlp_b_s = din("gmlp_b_s", [8, 128])
    gmlp_w_out = din("gmlp_w_out", [D, D])
    sconv_w_in = din("sconv_w_in", [D, 3 * D])
    sconv_w_out = din("sconv_w_out", [D, D])
    cconv_w_pw1 = din("cconv_w_pw1", [D, 2 * D])
    cconv_w_pw2 = din("cconv_w_pw2", [D, D])
    ffn_w_gate = din("ffn_w_gate", [DEPTH, D, DFF])
    ffn_w_up = din("ffn_w_up", [DEPTH, D, DFF])
    ffn_w_down = din("ffn_w_down", [DEPTH, DFF, D])

    o_y = dout("o_y", [1088, D])
    o_pool_p = dout("o_pool_p", [15, D])
    o_pool_s = dout("o_pool_s", [60, D])
    o_v = dout("o_v", [64, D])
    o_sconv_p = dout("o_sconv_p", [2, D])
    o_sconv_s = dout("o_sconv_s", [8, D])
    o_cconv_p = dout("o_cconv_p", [30, D])
    o_cconv_s = dout("o_cconv_s", [120, D])

    ctx = []

    def sb(name, shape, dt):
        cm = nc.sbuf_tensor(name, list(shape), dt)
        t = cm.__enter__()
        ctx.append(cm)
        return t

    X = sb("X", [128, KC, T], F32)
    H = sb("H", [128, KC, T], BF16)
    RING = sb("RING", [128, NSLOT, SLOT_ELEMS], BF16)
    SCR = sb("SCR", [128, SCR_WORDS], F32)
    VEC = sb("VEC", [128, NV], F32)
    META = sb("META", [128, 65], F32)
    IDF = sb("IDF", [128, 128], F32)
    IDB = sb("IDB", [128, 128], BF16)
    ONEB = sb("ONEB", [128, 128], BF16)
    ONEF = sb("ONEF", [128, 128], F32)
    WST = sb("WST", [128, 8, 128], BF16)
    BSB = sb("BSB", [128, 8, 128], F32)
    BDB = sb("BDB", [64, 8, 64], BF16)
    GHB = sb("GHB", [128, KC, 30], BF16)
    cm = nc.psum_tensor("PS", [128, 8, 512], F32)
    PS = cm.__enter__()
    ctx.append(cm)

    def vcol(name, k):
        o = VOFF[name] + k
        return VEC[:, o:o + 1]

    st = {"bank": 0, "slot": 0, "scr": 0, "held": set()}
    slot_ch = [S.new_channel() for _ in range(NSLOT)]

    def psb(hold=False):
        b = st["bank"]
        while b in st["held"]:
            b = (b + 1) % 8
        st["bank"] = (b + 1) % 8
        if hold:
            st["held"].add(b)
        return b

    def release(*banks):
        for b in banks:
            st["held"].discard(b)

    def scr_reset():
        st["scr"] = 0

    def scr_f32(words):
        a = st["scr"]
        st["scr"] = a + words
        assert st["scr"] <= SCR_WORDS, ("scr overflow", st["scr"])
        return SCR[:, a:a + words]

    def scr_bf16(elems):
        w = (elems + 1) // 2
        return scr_f32(w).bitcast(BF16)

    BAR = ["__bar"]

    def load_w(view):
        s = st["slot"]
        st["slot"] = (s + 1) % NSLOT
        nk, ncols = view.shape[1], view.shape[2]
        assert nk * ncols <= SLOT_ELEMS
        dst = RING[:, s, 0:nk * ncols].rearrange("p (k n) -> p k n", k=nk)
        key = ("ring", s)
        S.add("pool", lambda e, dst=dst, view=view: e.dma_start(out=dst, in_=view),
              writes=[key], dma_ch=slot_ch[s], name="wload")
        return dst, key

    def wview(w2d, r0, nk, c0, ncols):
        return w2d[r0:r0 + nk * 128, c0:c0 + ncols].rearrange("(k p) n -> p k n", p=128)

    def kx(k, c0, c1):
        return [("x", k, b) for b in bk(c0, c1)]

    def kh(k, c0, c1):
        return [("h", k, b) for b in bk(c0, c1)]

    gen_ch = [S.new_channel() for _ in range(6)]
    gst = {"i": 0}

    def dma_sp(out, in_, reads=(), writes=(), is_out=False, slow=False):
        ch = gen_ch[gst["i"] % len(gen_ch)]
        gst["i"] += 1
        kw = {"allow_slow_non_contiguous": True} if slow else {}
        op = S.add("sp", lambda e: e.dma_start(out=out, in_=in_, **kw), reads=reads, writes=writes, dma_ch=ch,
                   name="dma")
        if is_out:
            S.out_dmas.append(op)
        return op

    def mm(out, lhsT, rhs, start, stop, reads, writes, skip=False):
        if skip:
            return S.add("pe", lambda e: e.matmul(out, lhsT=lhsT, rhs=rhs, start=start, stop=stop,
                                                  skip_group_check=True),
                         reads=reads, writes=writes, name="mm")
        return S.add("pe", lambda e: e.matmul(out, lhsT=lhsT, rhs=rhs, start=start, stop=stop),
                     reads=reads, writes=writes, name="mm")

    def tr(out, in_, ident, reads, writes):
        return S.add("pe", lambda e: e.transpose(out, in_, ident), reads=reads, writes=writes, name="tr")

    def is_small(ap):
        try:
            return int(ap.free_size()) <= 160
        except Exception:
            return False

    def act(out, in_, func, reads, writes, bias=None, scale=None, strict=()):
        kw = {}
        if bias is not None:
            kw["bias"] = bias
        if scale is not None:
            kw["scale"] = scale
        return S.add("act", lambda e: e.activation(out=out, in_=in_, func=func, **kw),
                     reads=reads, writes=writes, strict=strict, name="act", small=is_small(out))

    def tt(out, in0, in1, op, reads, writes, eng="dve"):
        return S.add(eng, lambda e: e.tensor_tensor(out=out, in0=in0, in1=in1, op=op),
                     reads=reads, writes=writes, name="tt", small=is_small(out))

    def ts(out, in0, s1, s2, op0, op1, reads, writes, strict=(), eng="dve"):
        if op1 is None:
            return S.add(eng, lambda e: e.tensor_scalar(out=out, in0=in0, scalar1=s1, scalar2=None, op0=op0),
                         reads=reads, writes=writes, strict=strict, name="ts", small=is_small(out))
        return S.add(eng, lambda e: e.tensor_scalar(out=out, in0=in0, scalar1=s1, scalar2=s2, op0=op0, op1=op1),
                     reads=reads, writes=writes, strict=strict, name="ts", small=is_small(out))

    def stt(out, in0, scalar, in1, op0, op1, reads, writes, strict=()):
        return S.add("dve", lambda e: e.scalar_tensor_tensor(out=out, in0=in0, scalar=scalar, in1=in1,
                                                             op0=op0, op1=op1),
                     reads=reads, writes=writes, strict=strict, name="stt", small=is_small(out))

    def cp(out, in_, reads, writes, eng="dve"):
        if eng == "act":
            return S.add(eng, lambda e: e.activation(out=out, in_=in_, func=AF.Copy), reads=reads, writes=writes,
                         name="cp", small=is_small(out))
        return S.add(eng, lambda e: e.tensor_copy(out=out, in_=in_), reads=reads, writes=writes, name="cp",
                     small=is_small(out))

    def memset(ap, val, writes, eng="dve", reads=()):
        return S.add(eng, lambda e: e.memset(ap, val), reads=reads, writes=writes, name="memset",
                     small=is_small(ap))

    def load_T(rows_ap, n, dest, dest_keys, stage, stage_key):
        dma_sp(stage[0:n, :], rows_ap, reads=BAR, writes=[stage_key])
        for j in range(4):
            b = psb()
            for q in range(4):
                k = 4 * j + q
                tr(PS[:, b, q * 128:q * 128 + n], stage[0:n, k * 128:(k + 1) * 128], IDF[0:n, 0:n],
                   reads=[stage_key, "idf"], writes=[("ps", b)])
            src = PS[:, b, :].rearrange("p (q c) -> p q c", q=4)[:, :, 0:n]
            cp(dest[:, 4 * j:4 * j + 4, :], src, reads=[("ps", b)],
               writes=[kk for q in range(4) for kk in dest_keys(4 * j + q)],
               eng=("dve" if j % 2 == 0 else "act"))

    def store_T(src, n, keys_of, outs, stage, stage_key):
        for j in range(4):
            b = psb()
            for q in range(4):
                k = 4 * j + q
                tr(PS[0:n, b, q * 128:(q + 1) * 128], src[:, k, :], IDF[:, :],
                   reads=list(keys_of(k)) + ["idf"], writes=[("ps", b)])
            cp(stage[0:n, j * 512:(j + 1) * 512], PS[0:n, b, :], reads=[("ps", b)] + BAR, writes=[(stage_key, j)],
               eng=("dve" if j % 2 == 0 else "act"))
        for (dap, r0, r1) in outs:
            dma_sp(dap, stage[r0:r1, :], reads=[(stage_key, j) for j in range(4)], is_out=True)

    def rmsnorm(gname, RSTD, SQ, out_fn, keep=False):
        assert not st["held"]
        st["bank"] = 0
        banks = [psb(True) for _ in NTILES]
        assert banks == [0, 1, 2]
        for k in range(KC):
            sq = SQ[k % 2]
            act(sq[:, 0:T], X[:, k, 0:T], AF.Square, reads=kx(k, 0, T) + BAR, writes=[("sq", k % 2)])
            for t, (a, b_) in enumerate(NTILES):
                mm(PS[:, banks[t], 0:b_ - a], ONEB[:, :], sq[:, a:b_], k == 0, k == KC - 1,
                   reads=[("sq", k % 2), "oneb"], writes=[("ps", banks[t])])
        for t, (a, b_) in enumerate(NTILES):
            act(PS[:, banks[t], 0:b_ - a], PS[:, banks[t], 0:b_ - a], AF.Ln, reads=[("ps", banks[t])] + BAR,
                writes=[("ps", banks[t])], scale=1.0 / D, bias=EPSB[:, 0:1])
            act(PS[:, banks[t], 0:b_ - a], PS[:, banks[t], 0:b_ - a], AF.Exp, reads=[("ps", banks[t])],
                writes=[("ps", banks[t])], scale=-0.5)
        for k in range(KC):
            out_fn(k)
        if not keep:
            release(*banks)

    RS_PS = PS[:, 0:3, :].rearrange("p b c -> p (b c)")
    RSK = [("ps", 0), ("ps", 1), ("ps", 2)]

    def norm_to_H(gname, RSTD):
        def f(k):
            stt(H[:, k, 0:T], X[:, k, 0:T], vcol(gname, k), RS_PS[:, 0:T], ALU.mult, ALU.mult,
                reads=kx(k, 0, T) + RSK, writes=kh(k, 0, T))
        return f

    def proj_out_partial(w2d, r0, nkf, rhs_fn, rhs_keys, cols_tiles, evac):
        assert nkf * 1024 <= SLOT_ELEMS
        for half in range(2):
            wp, wk = load_w(wview(w2d, r0, nkf, half * 1024, 1024))
            for mo8 in range(8):
                mo = half * 8 + mo8
                banks = [psb() for _ in cols_tiles]
                for kf in range(nkf):
                    for t, (a, b_) in enumerate(cols_tiles):
                        mm(PS[:, banks[t], 0:b_ - a], wp[:, kf, mo8 * 128:(mo8 + 1) * 128], rhs_fn(kf, a, b_),
                           kf == 0, kf == nkf - 1, reads=[wk] + rhs_keys(kf, a, b_), writes=[("ps", banks[t])])
                for t, (a, b_) in enumerate(cols_tiles):
                    evac(mo, banks[t], a, b_)

    def x_add_evac(mo, b, a, b_):
        tt(X[:, mo, a:b_], PS[:, b, 0:b_ - a], X[:, mo, a:b_], ALU.add,
           reads=[("ps", b)], writes=kx(mo, a, b_))

    EPSB = sb("EPSB", [128, 1], F32)
    DUMMY = sb("DUMMY", [128, 2], F32)
    memset(EPSB[:, :], EPS, writes=["epsb"])
    S.barrier()
    dma_sp(VEC[:, :], vecs_d[:, :], writes=["vec"])
    dma_sp(META[:, :], meta_d[:, :], writes=["meta"])
    dma_sp(IDF[:, :], ident_d[:, :], writes=["idf"])
    memset(ONEB[:, :], 1.0, writes=["oneb"])
    memset(ONEF[:, :], 1.0, writes=["onef"])
    memset(GHB[:, :, :], 0.0, writes=[("gh", k) for k in range(KC)])
    cp(IDB[:, :], IDF[:, :], reads=["idf"], writes=["idb"])
    dma_sp(BSB[:, :, :], gmlp_b_s.partition_broadcast(128), writes=["bsb"])
    scr_reset()
    WSN = scr_f32(1024).rearrange("p (h j) -> p h j", h=8)
    BDF = scr_f32(512)
    BDF3 = BDF[0:64, :].rearrange("p (h i) -> p h i", h=8)
    dma_sp(WSN, gmlp_w_s.rearrange("h i j -> i h j"), reads=BAR, writes=["wsn"])
    memset(WSN[0:64, :, 64:128], 0.0, writes=["wsn"])
    memset(BDF[:, :], 0.0, writes=["bdf"], eng="pool")
    for s in range(4):
        for hd in range(8):
            dma_sp(BDF3[16 * s:16 * s + 16, hd, 16 * s:16 * s + 16],
                   gmlp_w_s[hd, 0:16, 0:16].rearrange("i j -> j i"), reads=BAR, writes=["bdf"], slow=True)
    cp(BDB[:, :, :], BDF3, reads=["bdf"], writes=["bdb"], eng="pool")
    for j in range(2):
        b = psb()
        for q in range(4):
            hd = 4 * j + q
            tr(PS[:, b, q * 128:(q + 1) * 128], WSN[:, hd, :], IDF[:, :], reads=["wsn", "idf"], writes=[("ps", b)])
        cp(WST[:, 4 * j:4 * j + 4, :], PS[:, b, :].rearrange("p (q c) -> p q c", q=4), reads=[("ps", b)],
           writes=["wst"])
    STG = [scr_f32(2048), scr_f32(2048)]
    blocks = [(r, 128) for r in range(0, 1152, 128)] + [(1152, 80)]
    for bi, (r0, n) in enumerate(blocks):
        load_T(xin[r0:r0 + n, :], n, X[:, :, r0:r0 + n], lambda k, r0=r0, n=n: kx(k, r0, r0 + n),
               STG[bi % 2], ("stg", bi % 2))

    def ffn(l, tiles=TILES):
        cb = tiles[0][0]
        S.barrier()
        scr_reset()
        RSTD = scr_f32(T)
        SQ = [scr_bf16(T), scr_bf16(T)]
        HFF = [scr_bf16(4 * 1216).rearrange("p (k t) -> p k t", k=4) for _ in range(2)]
        GSB = [scr_f32(1216) for _ in range(2)]
        rmsnorm("gffn%d" % l, RSTD, SQ, norm_to_H("gffn%d" % l, RSTD))
        wg, wu, wd = ffn_w_gate[l], ffn_w_up[l], ffn_w_down[l]
        nslab = DFF // 512
        gi = [0]

        def down(sl):
            hb = HFF[sl % 2]
            proj_out_partial(wd, sl * 512, 4,
                             lambda kf, a, b_: hb[:, kf, a - cb:b_ - cb],
                             lambda kf, a, b_: [("hff", sl % 2, kf)],
                             tiles, x_add_evac)

        for sl in range(nslab):
            hb = HFF[sl % 2]
            for half in range(2):
                wgp, wgk = load_w(wview(wg, 0, KC, sl * 512 + half * 256, 256))
                wup, wuk = load_w(wview(wu, 0, KC, sl * 512 + half * 256, 256))
                for mi in range(2):
                    m = half * 2 + mi
                    gs = GSB[gi[0] % 2]
                    gkey = ("gsb", gi[0] % 2)
                    gi[0] += 1
                    banks = [psb() for _ in tiles]
                    for k in range(KC):
                        for t, (a, b_) in enumerate(tiles):
                            mm(PS[:, banks[t], 0:b_ - a], wgp[:, k, mi * 128:(mi + 1) * 128], H[:, k, a:b_],
                               k == 0, k == KC - 1, reads=[wgk] + kh(k, a, b_), writes=[("ps", banks[t])])
                    for t, (a, b_) in enumerate(tiles):
                        act(gs[:, a - cb:b_ - cb], PS[:, banks[t], 0:b_ - a], AF.Silu,
                            reads=[("ps", banks[t])] + BAR, writes=[gkey])
                    banks2 = [psb() for _ in tiles]
                    for k in range(KC):
                        for t, (a, b_) in enumerate(tiles):
                            mm(PS[:, banks2[t], 0:b_ - a], wup[:, k, mi * 128:(mi + 1) * 128], H[:, k, a:b_],
                               k == 0, k == KC - 1, reads=[wuk] + kh(k, a, b_), writes=[("ps", banks2[t])])
                    for t, (a, b_) in enumerate(tiles):
                        tt(hb[:, m, a - cb:b_ - cb], PS[:, banks2[t], 0:b_ - a], gs[:, a - cb:b_ - cb], ALU.mult,
                           reads=[("ps", banks2[t]), gkey] + BAR, writes=[("hff", sl % 2, m)])
            if sl >= 1:
                down(sl - 1)
        down(nslab - 1)

    def pool_mixer():
        S.barrier()
        scr_reset()
        W = TP + 124
        SQ = [scr_bf16(T), scr_bf16(T)]
        STGp = scr_f32(2048)
        HIST = scr_f32(KC * 60).rearrange("p (k c) -> p k c", k=KC)
        DIFF = scr_bf16(4 * 1216).rearrange("p (k t) -> p k t", k=4)
        PST = scr_f32(KC * 75).rearrange("p (k c) -> p k c", k=KC)
        HPs = [scr_bf16(4 * W).rearrange("p (k c) -> p k c", k=4)] * 2
        OST = STGp
        rmsnorm("gmix0", None, SQ, lambda k: None, keep=True)
        load_T(st_pool[:, :], 60, HIST, lambda k: [("hist", k)], STGp, "stgp")

        def sv(ap2d):
            return ap2d.rearrange("p (s c) -> p s c", s=4)

        ptiles = [(16, 528), (528, 1040), (1040, TP)]
        for g in range(4):
            w = 2 << g
            HP = HPs[g % 2]

            def v3(c4, o, n):
                return HP[:, c4, TP:TP + 124].rearrange("p (s c) -> p s c", s=4)[:, :, o:o + n]

            for c4 in range(4):
                c = 4 * g + c4
                hk = ("hp", c4)
                stt(HP[:, c4, 0:TP], X[:, c, 0:TP], vcol("gmix0", c), RS_PS[:, 0:TP], ALU.mult, ALU.mult,
                    reads=kx(c, 0, TP) + RSK + BAR, writes=[hk])
                stt(v3(c4, 15, 16), sv(X[:, c, TP:T]), vcol("gmix0", c), sv(RS_PS[:, TP:T]), ALU.mult, ALU.mult,
                    reads=kx(c, TP, T) + RSK + BAR, writes=[hk])
                cp(v3(c4, 0, 15), HIST[:, c, :].rearrange("p (s c) -> p s c", s=4), reads=[("hist", c)] + BAR,
                   writes=[hk], eng="act")
                stt(PST[:, c, 0:15], X[:, c, TP - 15:TP], vcol("gmix0", c), RS_PS[:, TP - 15:TP], ALU.mult, ALU.mult,
                    reads=kx(c, 0, TP) + RSK + BAR, writes=[("pst", c)])
                stt(PST[:, c, 15:75].rearrange("p (s c) -> p s c", s=4), sv(X[:, c, TP:T])[:, :, 1:16],
                    vcol("gmix0", c), sv(RS_PS[:, TP:T])[:, :, 1:16], ALU.mult, ALU.mult,
                    reads=kx(c, TP, T) + RSK + BAR, writes=[("pst", c)])
                banks = [psb() for _ in ptiles]
                for i_ in range(w):
                    for t, (a, b_) in enumerate(ptiles):
                        mm(PS[:, banks[t], 0:b_ - a], IDB[:, :], HP[:, c4, a - i_:b_ - i_], i_ == 0, i_ == w - 1,
                           reads=[hk, "idb"], writes=[("ps", banks[t])])
                    mm(PS[:, banks[2], 128:192], IDB[:, :], v3(c4, 15 - i_, 16), False, i_ == w - 1,
                       reads=[hk, "idb"], writes=[("ps", banks[2])], skip=True)
                tt(PS[:, banks[0], MAIN0 - 16:MAIN0], PS[:, banks[0], MAIN0 - 16:MAIN0], META[:, g * 16:(g + 1) * 16],
                   ALU.mult, reads=[("ps", banks[0]), "meta"], writes=[("ps", banks[0])])
                for t, (a, b_) in enumerate(ptiles):
                    stt(DIFF[:, c4, a - C0:b_ - C0], PS[:, banks[t], 0:b_ - a], 1.0 / w, HP[:, c4, a:b_],
                        ALU.mult, ALU.subtract, reads=[("ps", banks[t]), hk] + BAR, writes=[("diff", c4)])
                stt(sv(DIFF[:, c4, TP - C0:T - C0]), sv(PS[:, banks[2], 128:192]), 1.0 / w, v3(c4, 15, 16),
                    ALU.mult, ALU.subtract, reads=[("ps", banks[2]), hk] + BAR, writes=[("diff", c4)])
            wp, wk = load_w(pool_w[g].rearrange("(k p) n -> p k n", p=128))
            for mo in range(4):
                banks = [psb() for _ in TILES]
                for kc in range(4):
                    for t, (a, b_) in enumerate(TILES):
                        mm(PS[:, banks[t], 0:b_ - a], wp[:, kc, mo * 128:(mo + 1) * 128],
                           DIFF[:, kc, a - C0:b_ - C0], kc == 0, kc == 3,
                           reads=[wk, ("diff", kc)], writes=[("ps", banks[t])])
                for t, (a, b_) in enumerate(TILES):
                    ch = 4 * g + mo
                    stt(X[:, ch, a:b_], PS[:, banks[t], 0:b_ - a], vcol("pscale", ch), X[:, ch, a:b_],
                        ALU.mult, ALU.add, reads=[("ps", banks[t]), "vec"], writes=kx(ch, a, b_))
        release(0, 1, 2)
        store_T(PST, 75, lambda k: [("pst", k)], [(o_pool_p[:, :], 0, 15), (o_pool_s[:, :], 15, 75)], OST, "ost")

    def ln_finish(G, bsum, bsq, MEAN, RS, MSQ, msqkey="msq"):
        act(MEAN[:, 0:G], PS[:, bsum, 0:G], AF.Copy, reads=[("ps", bsum)] + BAR, writes=["mean"], scale=1.0 / D)
        tt(MSQ[:, 0:G], MEAN[:, 0:G], MEAN[:, 0:G], ALU.mult, reads=["mean"] + BAR, writes=[msqkey])
        stt(PS[:, bsq, 0:G], PS[:, bsq, 0:G], 1.0 / D, MSQ[:, 0:G], ALU.mult, ALU.subtract,
            reads=[("ps", bsq), msqkey] + BAR, writes=[("ps", bsq)])
        act(PS[:, bsq, 0:G], PS[:, bsq, 0:G], AF.Ln, reads=[("ps", bsq)], writes=[("ps", bsq)], bias=EPSB[:, 0:1],
            strict=["epsb"])
        act(PS[:, bsq, 0:G], PS[:, bsq, 0:G], AF.Exp, reads=[("ps", bsq)], writes=[("ps", bsq)], scale=-0.5)
        stt(PS[:, bsum, 0:G], MEAN[:, 0:G], -1.0, PS[:, bsq, 0:G], ALU.mult, ALU.mult,
            reads=["mean", ("ps", bsq)] + BAR, writes=[("ps", bsum)])

    def gmlp_mixer():
        S.barrier()
        scr_reset()
        RSTD = scr_f32(T)
        SQ = [scr_bf16(T), scr_bf16(T)]
        rmsnorm("gmix1", RSTD, SQ, norm_to_H("gmix1", RSTD))
        GM = 384
        for gi_, (c0, c1) in enumerate(GROUPS):
            G = c1 - c0
            is_s = (gi_ == 3)
            S.barrier()
            scr_reset()
            BIG = scr_f32(KC * G).rearrange("p (k c) -> p k c", k=KC)
            VB = scr_bf16(2048)
            CSQ = [scr_f32(GM), scr_f32(GM)]
            MEAN = scr_f32(GM)
            RS = scr_f32(GM)
            MSQ = CSQ[0]
            VNB = [scr_bf16(GM), scr_bf16(GM)]
            UT = [scr_f32(GM), scr_f32(GM)]
            GS = [scr_bf16(4 * GM).rearrange("p (k c) -> p k c", k=4) for _ in range(2)]
            if is_s:
                VST = scr_f32(KC * 64).rearrange("p (k c) -> p k c", k=KC)
                OSTv = scr_f32(2048)
            bsum, bsq = psb(True), psb(True)
            pend = None

            def stats(f, first, last):
                cs = CSQ[f % 2]
                act(cs[:, 0:G], BIG[:, f, 0:G], AF.Square, reads=[("big", f)] + BAR, writes=[("csq", f % 2)])
                mm(PS[:, bsum, 0:G], ONEF[:, :], BIG[:, f, 0:G], first, last,
                   reads=[("big", f), "onef"], writes=[("ps", bsum)])
                mm(PS[:, bsq, 0:G], ONEF[:, :], cs[:, 0:G], first, last,
                   reads=[("csq", f % 2), "onef"], writes=[("ps", bsq)])

            for f in range(KC):
                if f % 2 == 0:
                    wp, wk = load_w(wview(gmlp_w_in, 0, KC, D + f * 128, 256))
                b = psb()
                for k in range(KC):
                    mm(PS[:, b, 0:G], wp[:, k, (f % 2) * 128:(f % 2 + 1) * 128], H[:, k, c0:c1], k == 0, k == KC - 1,
                       reads=[wk] + kh(k, c0, c1), writes=[("ps", b)])
                act(BIG[:, f, 0:G], PS[:, b, 0:G], AF.Gelu_apprx_tanh, reads=[("ps", b), "vec"] + BAR,
                    writes=[("big", f)], bias=vcol("bv", f))
                if pend is not None:
                    stats(pend, pend == 0, False)
                pend = f
            stats(pend, False, True)
            ln_finish(G, bsum, bsq, MEAN, RS, MSQ, msqkey=("csq", 0))
            ntile = (G + 127) // 128
            tw = [min(128, G - j * 128) for j in range(ntile)]
            PSB = PS[:, :, :].bitcast(BF16)
            tb = [[psb(True), psb(True)] for _ in range(ntile)]
            for f in range(KC):
                stt(BIG[:, f, 0:G], BIG[:, f, 0:G], vcol("lng", f), PS[:, bsq, 0:G], ALU.mult, ALU.mult,
                    reads=[("big", f), ("ps", bsq), "vec"] + BAR, writes=[("big", f)])
                stt(BIG[:, f, 0:G], PS[:, bsum, 0:G], vcol("lng", f), BIG[:, f, 0:G], ALU.mult, ALU.add,
                    reads=[("big", f), ("ps", bsum), "vec"] + BAR, writes=[("big", f)])
                vn = VNB[f % 2]
                act(vn[:, 0:G], BIG[:, f, 0:G], AF.Identity, reads=[("big", f), "vec"] + BAR,
                    writes=[("vnb", f % 2)], bias=vcol("lnb", f))
                if is_s:
                    act(VST[:, f, 0:64], BIG[:, f, 0:G], AF.Identity, reads=[("big", f), "vec"] + BAR,
                        writes=[("vst", f)], bias=vcol("lnb", f))
                for j in range(ntile):
                    bb = tb[j][f // 8]
                    tr(PSB[0:tw[j], bb, (f % 8) * 128:(f % 8 + 1) * 128], vn[:, j * 128:j * 128 + tw[j]], IDB[:, :],
                       reads=[("vnb", f % 2), "idb"], writes=[("ps", bb)])
            release(bsum, bsq)
            for j in range(ntile):
                n = tw[j]
                for hh in range(2):
                    cp(VB[0:n, hh * 1024:(hh + 1) * 1024], PSB[0:n, tb[j][hh], :], reads=[("ps", tb[j][hh])] + BAR,
                       writes=[("vb", hh)], eng=("dve" if hh == 0 else "act"))
                release(tb[j][0], tb[j][1])
                for q in range(4):
                    b = psb()
                    for fi in range(4):
                        f = 4 * q + fi
                        hd = f // 2
                        rhs = BDB[0:n, hd, 0:n] if is_s else WST[0:n, hd, 0:n]
                        mm(PS[:, b, fi * 128:fi * 128 + n], VB[0:n, f * 128:(f + 1) * 128], rhs, True, True,
                           reads=[("vb", f // 8), "bdb", "wst"], writes=[("ps", b)])
                    for hp in range(2):
                        hd = 2 * q + hp
                        for fi in range(2):
                            f = 4 * q + 2 * hp + fi
                            src = PS[:, b, (2 * hp + fi) * 128:(2 * hp + fi) * 128 + n]
                            if is_s:
                                for s_ in range(4):
                                    tt(BIG[:, f, 16 * s_:16 * s_ + 16], src[:, 16 * s_:16 * s_ + 16],
                                       BSB[:, hd, 0:16], ALU.add,
                                       reads=[("ps", b), "bsb"] + BAR, writes=[("big", f)])
                            else:
                                tt(BIG[:, f, j * 128:j * 128 + n], src, BSB[:, hd, 0:n], ALU.add,
                                   reads=[("ps", b), "bsb"] + BAR, writes=[("big", f)])
            u0 = max(c0, TILES_H[0][0])
            uo = u0 - c0
            Gu = c1 - u0
            for f in range(KC):
                if f % 2 == 0:
                    wp, wk = load_w(wview(gmlp_w_in, 0, KC, f * 128, 256))
                b = psb()
                for k in range(KC):
                    mm(PS[:, b, 0:Gu], wp[:, k, (f % 2) * 128:(f % 2 + 1) * 128], H[:, k, u0:c1], k == 0, k == KC - 1,
                       reads=[wk] + kh(k, u0, c1), writes=[("ps", b)])
                ut = UT[f % 2]
                act(ut[:, 0:Gu], PS[:, b, 0:Gu], AF.Gelu_apprx_tanh, reads=[("ps", b), "vec"] + BAR,
                    writes=[("ut", f % 2)], bias=vcol("bu", f))
                sl = f // 4
                gsb_ = GS[sl % 2]
                tt(gsb_[:, f % 4, 0:Gu], ut[:, 0:Gu], BIG[:, f, uo:G], ALU.mult,
                   reads=[("ut", f % 2), ("big", f)] + BAR, writes=[("gs", sl % 2, f % 4)])
                if f % 4 == 3:
                    proj_out_partial(gmlp_w_out, sl * 512, 4,
                                     lambda kf, a, b_, gsb_=gsb_: gsb_[:, kf, a - u0:b_ - u0],
                                     lambda kf, a, b_, sl=sl: [("gs", sl % 2, kf)],
                                     [(u0, c1)], x_add_evac)
            if is_s:
                store_T(VST, 64, lambda k: [("vst", k)], [(o_v[:, :], 0, 64)], OSTv, "ostv")

    def sconv_mixer(tiles=TILES_H):
        cb = tiles[0][0]
        S.barrier()
        scr_reset()
        SH2 = scr_f32(KC * 8).rearrange("p (k c) -> p k c", k=KC)
        keep = st["scr"]
        RSTD = scr_f32(T)
        SQ = [scr_bf16(T), scr_bf16(T)]
        STG2 = scr_f32(2048)
        rmsnorm("gmix2", RSTD, SQ, norm_to_H("gmix2", RSTD))
        load_T(st_sconv[:, :], 8, SH2, lambda k: [("sh2", k)], STG2, "stg2")
        S.barrier()
        st["scr"] = keep
        NP = TP - cb
        CS = scr_f32(T - cb)
        CXP = scr_f32(2 + NP)
        CXS = scr_f32(4 * 18).rearrange("p (s c) -> p s c", s=4)
        CVP = scr_f32(NP)
        CVS = scr_f32(64).rearrange("p (s c) -> p s c", s=4)
        SST = scr_f32(KC * 10).rearrange("p (k c) -> p k c", k=KC)
        Z = [scr_bf16(4 * 1216).rearrange("p (k t) -> p k t", k=4) for _ in range(2)]
        OST = scr_f32(2048)
        memset(CXP[:, 0:2], 0.0, writes=["cxp"], reads=BAR)
        wb = wc = wx = None
        for m in range(KC):
            if m % 2 == 0:
                wc = load_w(wview(sconv_w_in, 0, KC, D + m * 128, 256))
                wx = load_w(wview(sconv_w_in, 0, KC, 2 * D + m * 128, 256))
                wb = load_w(wview(sconv_w_in, 0, KC, m * 128, 256))
            mi = m % 2

            def proj(wpk):
                wp, wk = wpk
                banks = [psb() for _ in tiles]
                for k in range(KC):
                    for t, (a, b_) in enumerate(tiles):
                        mm(PS[:, banks[t], 0:b_ - a], wp[:, k, mi * 128:(mi + 1) * 128], H[:, k, a:b_],
                           k == 0, k == KC - 1, reads=[wk] + kh(k, a, b_), writes=[("ps", banks[t])])
                return banks

            bc = proj(wc)
            for t, (a, b_) in enumerate(tiles):
                cp(CS[:, a - cb:b_ - cb], PS[:, bc[t], 0:b_ - a], reads=[("ps", bc[t])] + BAR, writes=["cs"], eng="act")
            bx = proj(wx)
            for t, (a, b_) in enumerate(tiles):
                pe_ = min(b_, TP)
                tt(CXP[:, 2 + a - cb:2 + pe_ - cb], PS[:, bx[t], 0:pe_ - a], CS[:, a - cb:pe_ - cb], ALU.mult,
                   reads=[("ps", bx[t]), "cs"] + BAR, writes=["cxp"])
                if b_ > TP:
                    tt(CXS[:, :, 2:18], PS[:, bx[t], TP - a:T - a].rearrange("p (s c) -> p s c", s=4),
                       CS[:, TP - cb:T - cb].rearrange("p (s c) -> p s c", s=4), ALU.mult,
                       reads=[("ps", bx[t]), "cs"] + BAR, writes=["cxs"])
            cp(CXS[:, :, 0:2], SH2[:, m, :].rearrange("p (s c) -> p s c", s=4), reads=[("sh2", m)] + BAR,
               writes=["cxs"])
            ts(CXP[:, 2 + MAIN0 - cb - 2:2 + MAIN0 - cb], CXP[:, 2 + MAIN0 - cb - 2:2 + MAIN0 - cb], META[:, 64:65],
               None, ALU.mult, None, reads=["cxp"], writes=["cxp"], strict=["meta"])
            cp(SST[:, m, 0:2], CXP[:, 2 + NP - 2:2 + NP], reads=["cxp"] + BAR, writes=[("sst", m)], eng="act")
            cp(SST[:, m, 2:10].rearrange("p (s c) -> p s c", s=4), CXS[:, :, 16:18], reads=["cxs"] + BAR,
               writes=[("sst", m)], eng="act")
            ts(CVP[:, 0:NP], CXP[:, 0:NP], vcol("scw0", m), None, ALU.mult, None, reads=["cxp", "vec"] + BAR,
               writes=["cvp"])
            stt(CVP[:, 0:NP], CXP[:, 1:1 + NP], vcol("scw1", m), CVP[:, 0:NP], ALU.mult, ALU.add,
                reads=["cxp", "cvp"], writes=["cvp"])
            stt(CVP[:, 0:NP], CXP[:, 2:2 + NP], vcol("scw2", m), CVP[:, 0:NP], ALU.mult, ALU.add,
                reads=["cxp", "cvp"], writes=["cvp"])
            ts(CVS[:, :, :], CXS[:, :, 0:16], vcol("scw0", m), None, ALU.mult, None, reads=["cxs", "vec"] + BAR,
               writes=["cvs"])
            stt(CVS[:, :, :], CXS[:, :, 1:17], vcol("scw1", m), CVS[:, :, :], ALU.mult, ALU.add,
                reads=["cxs", "cvs"], writes=["cvs"])
            stt(CVS[:, :, :], CXS[:, :, 2:18], vcol("scw2", m), CVS[:, :, :], ALU.mult, ALU.add,
                reads=["cxs", "cvs"], writes=["cvs"])
            bb = proj(wb)
            sl = m // 4
            zb = Z[sl % 2]
            for t, (a, b_) in enumerate(tiles):
                pe_ = min(b_, TP)
                tt(zb[:, m % 4, a - cb:pe_ - cb], PS[:, bb[t], 0:pe_ - a], CVP[:, a - cb:pe_ - cb], ALU.mult,
                   reads=[("ps", bb[t]), "cvp"] + BAR, writes=[("z", sl % 2, m % 4)])
                if b_ > TP:
                    tt(zb[:, m % 4, TP - cb:T - cb].rearrange("p (s c) -> p s c", s=4),
                       PS[:, bb[t], TP - a:T - a].rearrange("p (s c) -> p s c", s=4), CVS[:, :, :], ALU.mult,
                       reads=[("ps", bb[t]), "cvs"] + BAR, writes=[("z", sl % 2, m % 4)])
            if m % 4 == 3:
                proj_out_partial(sconv_w_out, sl * 512, 4,
                                 lambda kf, a, b_, zb=zb: zb[:, kf, a - cb:b_ - cb],
                                 lambda kf, a, b_, sl=sl: [("z", sl % 2, kf)],
                                 tiles, x_add_evac)
        store_T(SST, 10, lambda k: [("sst", k)], [(o_sconv_p[:, :], 0, 2), (o_sconv_s[:, :], 2, 10)], OST, "ost2")

    def cconv_mixer():
        S.barrier()
        scr_reset()
        RSTD = scr_f32(T)
        SQ = [scr_bf16(T), scr_bf16(T)]
        rmsnorm("gmix3", RSTD, SQ, norm_to_H("gmix3", RSTD))
        dma_sp(o_cconv_s.rearrange("(s r) d -> s r d", r=30)[:, 0:14, :],
               st_cconv.rearrange("(s r) d -> s r d", r=30)[:, 16:30, :], is_out=True)
        GM = 384
        ND = 8
        for gi_, (c0, c1) in enumerate(GROUPS3):
            G = c1 - c0
            is_s = (gi_ == 3)
            S.barrier()
            scr_reset()
            BIG = scr_f32(KC * G).rearrange("p (k c) -> p k c", k=KC)
            CST = scr_f32(KC * 94).rearrange("p (k c) -> p k c", k=KC)
            SG2 = [scr_f32(GM), scr_f32(GM)]
            CSQ = [scr_f32(GM), scr_f32(GM)]
            MEAN = scr_f32(GM)
            RS = None
            MSQ = CSQ[0]
            DG = [scr_bf16(128) for _ in range(ND)]
            if is_s:
                GHS = scr_f32(KC * 120).rearrange("p (k c) -> p k c", k=KC)
                STG3 = scr_f32(2048)
                GLUB3s = [scr_bf16(4 * 46).rearrange("p (s c) -> p s c", s=4) for _ in range(2)]
                load_T(st_cconv[:, :], 120, GHS, lambda k: [("ghs", k)], STG3, "stg3")
            else:
                GLUBs = [scr_bf16(32 + GM) for _ in range(2)]
            bsum, bsq = psb(True), psb(True)
            pend = None
            dgi = [0]

            def stats(f, first, last):
                cs = CSQ[f % 2]
                act(cs[:, 0:G], BIG[:, f, 0:G], AF.Square, reads=[("big", f)] + BAR, writes=[("csq", f % 2)])
                mm(PS[:, bsum, 0:G], ONEF[:, :], BIG[:, f, 0:G], first, last,
                   reads=[("big", f), "onef"], writes=[("ps", bsum)])
                mm(PS[:, bsq, 0:G], ONEF[:, :], cs[:, 0:G], first, last,
                   reads=[("csq", f % 2), "onef"], writes=[("ps", bsq)])

            wst = {}

            def pw1_part(m):
                if m % 2 == 0:
                    wst["g"] = load_w(wview(cconv_w_pw1, 0, KC, D + m * 128, 256))
                    wst["a"] = load_w(wview(cconv_w_pw1, 0, KC, m * 128, 256))
                wgp, wgk = wst["g"]
                wap, wak = wst["a"]
                mi = m % 2
                SGm = SG2[m % 2]
                sgk = ("sg", m % 2)
                gk = ("glu", m % 2)
                bg_ = psb()
                for k in range(KC):
                    mm(PS[:, bg_, 0:G], wgp[:, k, mi * 128:(mi + 1) * 128], H[:, k, c0:c1], k == 0, k == KC - 1,
                       reads=[wgk] + kh(k, c0, c1), writes=[("ps", bg_)])
                act(SGm[:, 0:G], PS[:, bg_, 0:G], AF.Sigmoid, reads=[("ps", bg_), "vec"] + BAR, writes=[sgk],
                    bias=vcol("bg", m))
                ba_ = psb()
                for k in range(KC):
                    mm(PS[:, ba_, 0:G], wap[:, k, mi * 128:(mi + 1) * 128], H[:, k, c0:c1], k == 0, k == KC - 1,
                       reads=[wak] + kh(k, c0, c1), writes=[("ps", ba_)])
                if is_s:
                    GL3 = GLUB3s[m % 2]
                    cp(GL3[:, :, 0:30], GHS[:, m, :].rearrange("p (s c) -> p s c", s=4), reads=[("ghs", m)] + BAR,
                       writes=[gk], eng="act")
                    stt(GL3[:, :, 30:46], PS[:, ba_, 0:64].rearrange("p (s c) -> p s c", s=4), vcol("ba", m),
                        SGm[:, 0:64].rearrange("p (s c) -> p s c", s=4), ALU.add, ALU.mult,
                        reads=[("ps", ba_), sgk, "vec"] + BAR, writes=[gk])
                    stt(CST[:, m, 30:94], PS[:, ba_, 0:64], vcol("ba", m), SGm[:, 0:64], ALU.add, ALU.mult,
                        reads=[("ps", ba_), sgk, "vec"] + BAR, writes=[("cst", m)])
                else:
                    GL = GLUBs[m % 2]
                    cp(GL[:, 0:30], GHB[:, m, :], reads=[("gh", m)] + BAR, writes=[gk], eng="act")
                    stt(GL[:, 30:30 + G], PS[:, ba_, 0:G], vcol("ba", m), SGm[:, 0:G], ALU.add, ALU.mult,
                        reads=[("ps", ba_), sgk, "vec"] + BAR, writes=[gk])
                    if gi_ == 2:
                        stt(CST[:, m, 0:30], PS[:, ba_, G - 30:G], vcol("ba", m), SGm[:, G - 30:G], ALU.add, ALU.mult,
                            reads=[("ps", ba_), sgk, "vec"] + BAR, writes=[("cst", m)])
                    if gi_ == 0:
                        lo = 30 + (MAIN0 - 30 - c0)
                        ts(GL[:, lo:lo + 30], GL[:, lo:lo + 30], META[:, 64:65], None, ALU.mult, None,
                           reads=[gk], writes=[gk], strict=["meta"])
                    cp(GHB[:, m, :], GL[:, G:G + 30], reads=[gk], writes=[("gh", m)], eng="act")

            def conv_part(m):
                gk = ("glu", m % 2)
                bc_ = psb()
                for tap in range(31):
                    di = dgi[0] % ND
                    dgi[0] += 1
                    dg = DG[di]
                    if tap % 3 == 2:
                        S.add("act", lambda e, dg=dg, tap=tap, m=m: e.activation(
                            out=dg[:, 0:128], in_=IDF[:, :], func=AF.Copy, scale=vcol("dww%d" % tap, m)),
                            reads=["idf", "vec"] + BAR, writes=[("dg", di)], name="diag")
                    else:
                        S.add("dve", lambda e, dg=dg, tap=tap, m=m: e.tensor_scalar(
                            out=dg[:, 0:128], in0=IDF[:, :], scalar1=vcol("dww%d" % tap, m), scalar2=None,
                            op0=ALU.mult),
                            reads=["idf", "vec"] + BAR, writes=[("dg", di)], name="diag")
                    if is_s:
                        rhs = GLUB3s[m % 2][:, :, tap:tap + 16]
                    else:
                        rhs = GLUBs[m % 2][:, tap:tap + G]
                    mm(PS[:, bc_, 0:G], dg[:, 0:128], rhs, tap == 0, tap == 30,
                       reads=[("dg", di), gk], writes=[("ps", bc_)])
                act(BIG[:, m, 0:G], PS[:, bc_, 0:G], AF.Identity, reads=[("ps", bc_), "vec"] + BAR,
                    writes=[("big", m)], bias=vcol("dwb", m))

            for m in range(KC + 1):
                if m < KC:
                    pw1_part(m)
                if m >= 1:
                    conv_part(m - 1)
                    if pend is not None:
                        stats(pend, pend == 0, False)
                    pend = m - 1
            stats(pend, False, True)
            ln_finish(G, bsum, bsq, MEAN, RS, MSQ, msqkey=("csq", 0))
            for m in range(KC):
                tt(BIG[:, m, 0:G], BIG[:, m, 0:G], PS[:, bsq, 0:G], ALU.mult, reads=[("big", m), ("ps", bsq)] + BAR,
                   writes=[("big", m)])
                tt(BIG[:, m, 0:G], BIG[:, m, 0:G], PS[:, bsum, 0:G], ALU.add, reads=[("big", m), ("ps", bsum)] + BAR,
                   writes=[("big", m)])
                act(H[:, m, c0:c1], BIG[:, m, 0:G], AF.Silu, reads=[("big", m), "vec"] + BAR, writes=kh(m, c0, c1),
                    scale=vcol("clg", m), bias=vcol("clb", m))
            release(bsum, bsq)
            if gi_ == 2:
                S.barrier()
                st["scr"] = 0
                OSTa = scr_f32(2048)
                assert KC * G >= 2048
                store_T(CST[:, :, 0:30], 30, lambda k: [("cst", k)], [(o_cconv_p[:, :], 0, 30)], OSTa, "ost3a")
            if is_s:
                S.barrier()
                OSTb = STG3
                store_T(CST[:, :, 30:94], 64, lambda k: [("cst", k)],
                        [(o_cconv_s.rearrange("(s r) d -> s r d", r=30)[s_, 14:30, :], 16 * s_, 16 * s_ + 16)
                         for s_ in range(4)], OSTb, "ost3b")
        tiles = TILES_M
        for mo in range(KC):
            if mo % 2 == 0:
                wp, wk = load_w(wview(cconv_w_pw2, 0, KC, mo * 128, 256))
            banks = [psb() for _ in tiles]
            for k in range(KC):
                for t, (a, b_) in enumerate(tiles):
                    mm(PS[:, banks[t], 0:b_ - a], wp[:, k, (mo % 2) * 128:(mo % 2 + 1) * 128], H[:, k, a:b_],
                       k == 0, k == KC - 1, reads=[wk] + kh(k, a, b_), writes=[("ps", banks[t])])
            for t, (a, b_) in enumerate(tiles):
                stt(X[:, mo, a:b_], PS[:, banks[t], 0:b_ - a], vcol("b2", mo), X[:, mo, a:b_], ALU.add, ALU.add,
                    reads=[("ps", banks[t]), "vec"], writes=kx(mo, a, b_))

    def final_out():
        S.barrier()
        scr_reset()
        RSTD = scr_f32(T)
        SQ = [scr_bf16(T), scr_bf16(T)]
        OSTs = [scr_f32(2048), scr_f32(2048)]

        def f(k):
            stt(X[:, k, MAIN0:T], X[:, k, MAIN0:T], vcol("gfin", k), RS_PS[:, MAIN0:T], ALU.mult, ALU.mult,
                reads=kx(k, MAIN0, T) + RSK, writes=kx(k, MAIN0, T))
        rmsnorm("gfin", RSTD, SQ, f)
        oblocks = [(MAIN0 + 128 * i, 128) for i in range(8)] + [(TP, 64)]
        for bi, (cb, n) in enumerate(oblocks):
            store_T(X[:, :, cb:cb + n], n, lambda k, cb=cb, n=n: kx(k, cb, cb + n),
                    [(o_y[cb - MAIN0:cb - MAIN0 + n, :], 0, n)], OSTs[bi % 2], ("osty", bi % 2))

    stages = [("m0", pool_mixer), ("f0", lambda: ffn(0)), ("m1", gmlp_mixer), ("f1", lambda: ffn(1, TILES_H)),
              ("m2", sconv_mixer), ("f2", lambda: ffn(2, TILES_H)), ("m3", cconv_mixer),
              ("f3", lambda: ffn(3, TILES_M))]
    stopped = False
    for name_, fn_ in stages:
        fn_()
        if debug_stop == name_:
            dbg = nc.dram_tensor("dbg_x", [128, KC * T], F32, kind="ExternalOutput").ap()
            dbgh = nc.dram_tensor("dbg_h", [128, KC * T], BF16, kind="ExternalOutput").ap()
            S.barrier()
            dma_sp(dbg[:, :], X[:, :, :].rearrange("p k t -> p (k t)"), reads=BAR, is_out=True)
            dma_sp(dbgh[:, :], H[:, :, :].rearrange("p k t -> p (k t)"), reads=BAR, is_out=True)
            stopped = True
            break
    if not stopped:
        final_out()

    sems = {}
    sem_ctx = []
    for e in Sched.ENGS:
        cm_ = nc.semaphore("sem_" + e)
        sems[e] = cm_.__enter__()
        sem_ctx.append(cm_)
    dsem = []
    for i in range(S.nch):
        cm_ = nc.semaphore("dsem%d" % i)
        dsem.append(cm_.__enter__())
        sem_ctx.append(cm_)
    for e in Sched.ENGS:
        for op in S.ops[e]:
            for d in op.deps:
                if d.dma_ch is None:
                    d.sig = True
    cnt = {e: 0 for e in Sched.ENGS}
    for e in Sched.ENGS:
        for op in S.ops[e]:
            if op.sig:
                cnt[e] += 1
                op.sig = cnt[e]

    def emit(e, eng):
        waited = {}
        for op in S.ops[e]:
            for d in op.deps:
                if d.dma_ch is not None:
                    key, val, sem = ("d", d.dma_ch), 16 * d.dma_cnt, dsem[d.dma_ch]
                else:
                    key, val, sem = ("e", d.eng), d.sig, sems[d.eng]
                if waited.get(key, 0) >= val:
                    continue
                waited[key] = val
                eng.wait_ge(sem, val)
            if op.fn is None:
                ins = eng.memset(DUMMY[:, 0:1], 0.0)
            else:
                ins = op.fn(eng)
            if op.dma_ch is not None:
                ins.then_inc(dsem[op.dma_ch], 16)
            elif op.sig:
                ins.then_inc(sems[e], 1)
        if e == "sp":
            for ch in range(S.nch):
                if S.dma_cnt.get(ch, 0):
                    eng.wait_ge(dsem[ch], 16 * S.dma_cnt[ch])

    with nc.Block() as block:
        @block.tensor
        def _(eng):
            emit("pe", eng)

        @block.scalar
        def _(eng):
            emit("act", eng)

        @block.vector
        def _(eng):
            emit("dve", eng)

        @block.gpsimd
        def _(eng):
            emit("pool", eng)

        @block.sync
        def _(eng):
            emit("sp", eng)

    for cm_ in reversed(sem_ctx):
        cm_.__exit__(None, None, None)
    for cm_ in reversed(ctx):
        cm_.__exit__(None, None, None)
    stats = {e: len(S.ops[e]) for e in Sched.ENGS}
    return nc, stats


def _pack_vecs(inp):
    def col(v):
        return np.ascontiguousarray(np.asarray(v, np.float32).reshape(16, 128).T)
    parts = {}
    for i in range(4):
        parts["gmix%d" % i] = col(inp["norm_mix_g"][i])
        parts["gffn%d" % i] = col(inp["norm_ffn_g"][i])
    parts["gfin"] = col(inp["norm_final_g"])
    parts["pscale"] = col(inp["pool_scale"])
    parts["bu"] = col(inp["gmlp_b_in"][:D])
    parts["bv"] = col(inp["gmlp_b_in"][D:])
    parts["lng"] = col(inp["gmlp_ln_g"])
    parts["lnb"] = col(inp["gmlp_ln_b"])
    for i in range(3):
        parts["scw%d" % i] = col(inp["sconv_conv_w"][i])
    parts["ba"] = col(inp["cconv_b_pw1"][:D])
    parts["bg"] = col(inp["cconv_b_pw1"][D:])
    for i in range(31):
        parts["dww%d" % i] = col(inp["cconv_dw_w"][i])
    parts["dwb"] = col(inp["cconv_dw_b"])
    parts["clg"] = col(inp["cconv_ln_g"])
    parts["clb"] = col(inp["cconv_ln_b"])
    parts["b2"] = col(inp["cconv_b_pw2"])
    return np.ascontiguousarray(np.concatenate([parts[n] for n in VEC_NAMES], axis=1))


_CACHE = {}


def make_in_maps(inp):
    vecs = _pack_vecs(inp)
    ident = np.eye(128, dtype=np.float32)
    xp = inp["x_prompt"]
    xs = inp["x_sample"]
    shared = {
        "vecs": vecs, "ident": ident,
        "pool_w": inp["pool_w"], "gmlp_w_in": inp["gmlp_w_in"], "gmlp_w_s": inp["gmlp_w_s"],
        "gmlp_b_s": inp["gmlp_b_s"], "gmlp_w_out": inp["gmlp_w_out"], "sconv_w_in": inp["sconv_w_in"],
        "sconv_w_out": inp["sconv_w_out"], "cconv_w_pw1": inp["cconv_w_pw1"], "cconv_w_pw2": inp["cconv_w_pw2"],
        "ffn_w_gate": inp["ffn_w_gate"], "ffn_w_up": inp["ffn_w_up"], "ffn_w_down": inp["ffn_w_down"],
    }
    in_maps = []
    for c in range(NCORE):
        b, q = c // 4, c % 4
        p0 = q * 1024
        xin = np.zeros((T, D), np.float32)
        lo = p0 - (PRE + HALO)
        if lo >= 0:
            xin[0:TP] = xp[b, lo:lo + TP]
        else:
            xin[-lo:TP] = xp[b, 0:lo + TP]
        xin[TP:T] = xs[4 * c:4 * c + 4].reshape(64, D)
        meta = np.ones((128, 65), np.float32)
        if q == 0:
            for g, w in enumerate((2, 4, 8, 16)):
                for j in range(16):
                    meta[:, g * 16 + j] = float(w) / float(min(j + 1, w))
            meta[:, 64] = 0.0
        m = dict(shared)
        m["xin"] = xin
        m["st_pool"] = np.ascontiguousarray(inp["state_pool"][4 * c:4 * c + 4].reshape(60, D))
        m["st_sconv"] = np.ascontiguousarray(inp["state_sconv"][4 * c:4 * c + 4].reshape(8, D))
        m["st_cconv"] = np.ascontiguousarray(inp["state_cconv"][4 * c:4 * c + 4].reshape(120, D))
        m["meta"] = meta
        in_maps.append(m)
    return in_maps


def kernel(**inp):
    inp = {k: np.asarray(v) for k, v in inp.items()}
    if "nc" not in _CACHE:
        _CACHE["nc"] = build_program()
    nc, _ = _CACHE["nc"]
    in_maps = make_in_maps(inp)
    res = run_bass_kernel_spmd(nc, in_maps, core_ids=list(range(NCORE)))
    R = res.results
    y_prompt = np.zeros((2, 4096, D), np.float32)
    y_sample = np.zeros((32, 16, D), np.float32)
    pool_p = np.zeros((2, 15, D), np.float32)
    pool_s = np.zeros((32, 15, D), np.float32)
    v_s = np.zeros((32, 16, D), np.float32)
    sconv_p = np.zeros((2, 2, D), np.float32)
    sconv_s = np.zeros((32, 2, D), np.float32)
    cconv_p = np.zeros((2, 30, D), np.float32)
    cconv_s = np.zeros((32, 30, D), np.float32)
    for c in range(NCORE):
        b, q = c // 4, c % 4
        r = R[c]
        y_prompt[b, q * 1024:(q + 1) * 1024] = r["o_y"][0:1024]
        y_sample[4 * c:4 * c + 4] = r["o_y"][1024:1088].reshape(4, 16, D)
        pool_s[4 * c:4 * c + 4] = r["o_pool_s"].reshape(4, 15, D)
        v_s[4 * c:4 * c + 4] = r["o_v"].reshape(4, 16, D)
        sconv_s[4 * c:4 * c + 4] = r["o_sconv_s"].reshape(4, 2, D)
        cconv_s[4 * c:4 * c + 4] = r["o_cconv_s"].reshape(4, 30, D)
        if q == 3:
            pool_p[b] = r["o_pool_p"]
            sconv_p[b] = r["o_sconv_p"]
            cconv_p[b] = r["o_cconv_p"]
    return (y_prompt, y_sample, pool_p, pool_s, v_s, sconv_p, sconv_s, cconv_p, cconv_s)
```

```python
import numpy as np
import concourse.bass as bass
import concourse.mybir as mybir
from concourse.bass_utils import run_bass_kernel_spmd

F32 = mybir.dt.float32
BF16 = mybir.dt.bfloat16
F32R = mybir.dt.float32r
AF = mybir.ActivationFunctionType
ALU = mybir.AluOpType

D = 2048
KC = 16
DFF = 5632
NCORE = 8
DEPTH = 4
EPS = 1e-6
HALO = 128
PRE = 16
TP = PRE + HALO + 1024
T = TP + 64
C0 = PRE
MAIN0 = PRE + HALO
TILES = [(16, 528), (528, 1040), (1040, 1232)]
NTILES = [(0, 512), (512, 1024), (1024, 1232)]
GROUPS = [(16, 400), (400, 784), (784, 1168), (1168, 1232)]
TILES_H = [(112, 624), (624, 1136), (1136, 1232)]
TILES_M = [(144, 656), (656, 1168), (1168, 1232)]
GROUPS3 = [(114, 498), (498, 882), (882, 1168), (1168, 1232)]
BOUNDS = sorted(set([0, 16, 400, 528, 784, 1040, 1168, 1232, 112, 624, 1136, 144, 656, 114, 498, 882]))
NSLOT = 4
SLOT_ELEMS = 4096
SCR_WORDS = 11520

VEC_NAMES = (["gmix%d" % i for i in range(4)] + ["gffn%d" % i for i in range(4)] +
             ["gfin", "pscale", "bu", "bv", "lng", "lnb", "scw0", "scw1", "scw2", "ba", "bg"] +
             ["dww%d" % i for i in range(31)] + ["dwb", "clg", "clb", "b2"])
VOFF = {n: i * 16 for i, n in enumerate(VEC_NAMES)}
NV = len(VEC_NAMES) * 16


def bk(c0, c1):
    return [i for i in range(len(BOUNDS) - 1) if BOUNDS[i] < c1 and BOUNDS[i + 1] > c0]


class Op:
    __slots__ = ("eng", "fn", "deps", "sig", "dma_ch", "dma_cnt", "name", "small")

    def __init__(self, eng, fn, name=""):
        self.eng = eng
        self.fn = fn
        self.deps = []
        self.sig = None
        self.dma_ch = None
        self.dma_cnt = 0
        self.name = name
        self.small = False


class Sched:
    ENGS = ["pe", "act", "dve", "pool", "sp"]

    def __init__(self):
        self.ops = {e: [] for e in self.ENGS}
        self.lastw = {}
        self.readers = {}
        self.dma_cnt = {}
        self.dma_last = {}
        self.nch = 0
        self.out_dmas = []

    def new_channel(self):
        self.nch += 1
        return self.nch - 1

    def add(self, eng, fn, reads=(), writes=(), strict=(), dma_ch=None, name="", small=False):
        op = Op(eng, fn, name)
        op.small = small
        deps = {}

        def dep(o, st):
            if o is None or o is op:
                return
            if o.dma_ch is None and o.eng == eng and not st and not (o.small and eng != "pe"):
                return
            deps[id(o)] = o

        for k in reads:
            dep(self.lastw.get(k), isinstance(k, tuple) and k[0] == "ps" and eng != "pe")
        for k in strict:
            dep(self.lastw.get(k), True)
        for k in writes:
            dep(self.lastw.get(k), False)
            for r in self.readers.get(k, ()):
                dep(r, False)
        if dma_ch is not None:
            op.dma_ch = dma_ch
            self.dma_cnt[dma_ch] = self.dma_cnt.get(dma_ch, 0) + 1
            op.dma_cnt = self.dma_cnt[dma_ch]
            prev = self.dma_last.get(dma_ch)
            if prev is not None:
                deps[id(prev)] = prev
            self.dma_last[dma_ch] = op
        for k in list(reads) + list(strict):
            self.readers.setdefault(k, []).append(op)
        for k in writes:
            self.lastw[k] = op
            self.readers[k] = []
        op.deps = list(deps.values())
        self.ops[eng].append(op)
        return op

    def barrier(self, extra=()):
        def fn(e):
            return e.nop()
        op = Op("dve", None, "barrier")
        deps = {}
        for e in self.ENGS:
            if self.ops[e]:
                o = self.ops[e][-1]
                if e != "dve" or o.dma_ch is not None:
                    deps[id(o)] = o
        for ch, o in self.dma_last.items():
            deps[id(o)] = o
        op.deps = list(deps.values())
        self.ops["dve"].append(op)
        self.lastw["__bar"] = op
        self.readers["__bar"] = []
        self.bar = op
        return op


def build_program(debug_stop=None):
    nc = bass.Bass("TRN2", target_bir_lowering=False)
    S = Sched()

    def din(name, shape):
        return nc.dram_tensor(name, list(shape), F32, kind="ExternalInput").ap()

    def dout(name, shape):
        return nc.dram_tensor(name, list(shape), F32, kind="ExternalOutput").ap()

    xin = din("xin", [T, D])
    st_pool = din("st_pool", [60, D])
    st_sconv = din("st_sconv", [8, D])
    st_cconv = din("st_cconv", [120, D])
    vecs_d = din("vecs", [128, NV])
    meta_d = din("meta", [128, 65])
    ident_d = din("ident", [128, 128])
    pool_w = din("pool_w", [4, 512, 512])
    gmlp_w_in = din("gmlp_w_in", [D, 2 * D])
    gmlp_w_s = din("gmlp_w_s", [8, 128, 128])
    gmlp_b_s = din("gmlp_b_s", [8, 128])
    gmlp_w_out = din("gmlp_w_out", [D, D])
    sconv_w_in = din("sconv_w_in", [D, 3 * D])
    sconv_w_out = din("sconv_w_out", [D, D])
    cconv_w_pw1 = din("cconv_w_pw1", [D, 2 * D])
    cconv_w_pw2 = din("cconv_w_pw2", [D, D])
    ffn_w_gate = din("ffn_w_gate", [DEPTH, D, DFF])
    ffn_w_up = din("ffn_w_up", [DEPTH, D, DFF])
    ffn_w_down = din("ffn_w_down", [DEPTH, DFF, D])

    o_y = dout("o_y", [1088, D])
    o_pool_p = dout("o_pool_p", [15, D])
    o_pool_s = dout("o_pool_s", [60, D])
    o_v = dout("o_v", [64, D])
    o_sconv_p = dout("o_sconv_p", [2, D])
    o_sconv_s = dout("o_sconv_s", [8, D])
    o_cconv_p = dout("o_cconv_p", [30, D])
    o_cconv_s = dout("o_cconv_s", [120, D])

    ctx = []

    def sb(name, shape, dt):
        cm = nc.sbuf_tensor(name, list(shape), dt)
        t = cm.__enter__()
        ctx.append(cm)
        return t

    X = sb("X", [128, KC, T], F32)
    H = sb("H", [128, KC, T], BF16)
    RING = sb("RING", [128, NSLOT, SLOT_ELEMS], BF16)
    SCR = sb("SCR", [128, SCR_WORDS], F32)
    VEC = sb("VEC", [128, NV], F32)
    META = sb("META", [128, 65], F32)
    IDF = sb("IDF", [128, 128], F32)
    IDB = sb("IDB", [128, 128], BF16)
    ONEB = sb("ONEB", [128, 128], BF16)
    ONEF = sb("ONEF", [128, 128], F32)
    WST = sb("WST", [128, 8, 128], BF16)
    BSB = sb("BSB", [128, 8, 128], F32)
    BDB = sb("BDB", [64, 8, 64], BF16)
    GHB = sb("GHB", [128, KC, 30], BF16)
    cm = nc.psum_tensor("PS", [128, 8, 512], F32)
    PS = cm.__enter__()
    ctx.append(cm)

    def vcol(name, k):
        o = VOFF[name] + k
        return VEC[:, o:o + 1]

    st = {"bank": 0, "slot": 0, "scr": 0, "held": set()}
    slot_ch = [S.new_channel() for _ in range(NSLOT)]

    def psb(hold=False):
        b = st["bank"]
        while b in st["held"]:
            b = (b + 1) % 8
        st["bank"] = (b + 1) % 8
        if hold:
            st["held"].add(b)
        return b

    def release(*banks):
        for b in banks:
            st["held"].discard(b)

    def scr_reset():
        st["scr"] = 0

    def scr_f32(words):
        a = st["scr"]
        st["scr"] = a + words
        assert st["scr"] <= SCR_WORDS, ("scr overflow", st["scr"])
        return SCR[:, a:a + words]

    def scr_bf16(elems):
        w = (elems + 1) // 2
        return scr_f32(w).bitcast(BF16)

    BAR = ["__bar"]

    def load_w(view):
        s = st["slot"]
        st["slot"] = (s + 1) % NSLOT
        nk, ncols = view.shape[1], view.shape[2]
        assert nk * ncols <= SLOT_ELEMS
        dst = RING[:, s, 0:nk * ncols].rearrange("p (k n) -> p k n", k=nk)
        key = ("ring", s)
        S.add("pool", lambda e, dst=dst, view=view: e.dma_start(out=dst, in_=view),
              writes=[key], dma_ch=slot_ch[s], name="wload")
        return dst, key

    def wview(w2d, r0, nk, c0, ncols):
        return w2d[r0:r0 + nk * 128, c0:c0 + ncols].rearrange("(k p) n -> p k n", p=128)

    def kx(k, c0, c1):
        return [("x", k, b) for b in bk(c0, c1)]

    def kh(k, c0, c1):
        return [("h", k, b) for b in bk(c0, c1)]

    gen_ch = [S.new_channel() for _ in range(6)]
    gst = {"i": 0}

    def dma_sp(out, in_, reads=(), writes=(), is_out=False, slow=False):
        ch = gen_ch[gst["i"] % len(gen_ch)]
        gst["i"] += 1
        kw = {"allow_slow_non_contiguous": True} if slow else {}
        op = S.add("sp", lambda e: e.dma_start(out=out, in_=in_, **kw), reads=reads, writes=writes, dma_ch=ch,
                   name="dma")
        if is_out:
            S.out_dmas.append(op)
        return op

    def mm(out, lhsT, rhs, start, stop, reads, writes, skip=False):
        if skip:
            return S.add("pe", lambda e: e.matmul(out, lhsT=lhsT, rhs=rhs, start=start, stop=stop,
                                                  skip_group_check=True),
                         reads=reads, writes=writes, name="mm")
        return S.add("pe", lambda e: e.matmul(out, lhsT=lhsT, rhs=rhs, start=start, stop=stop),
                     reads=reads, writes=writes, name="mm")

    def tr(out, in_, ident, reads, writes):
        return S.add("pe", lambda e: e.transpose(out, in_, ident), reads=reads, writes=writes, name="tr")

    def is_small(ap):
        try:
            return int(ap.free_size()) <= 160
        except Exception:
            return False

    def act(out, in_, func, reads, writes, bias=None, scale=None, strict=()):
        kw = {}
        if bias is not None:
            kw["bias"] = bias
        if scale is not None:
            kw["scale"] = scale
        return S.add("act", lambda e: e.activation(out=out, in_=in_, func=func, **kw),
                     reads=reads, writes=writes, strict=strict, name="act", small=is_small(out))

    def tt(out, in0, in1, op, reads, writes, eng="dve"):
        return S.add(eng, lambda e: e.tensor_tensor(out=out, in0=in0, in1=in1, op=op),
                     reads=reads, writes=writes, name="tt", small=is_small(out))

    def ts(out, in0, s1, s2, op0, op1, reads, writes, strict=(), eng="dve"):
        if op1 is None:
            return S.add(eng, lambda e: e.tensor_scalar(out=out, in0=in0, scalar1=s1, scalar2=None, op0=op0),
                         reads=reads, writes=writes, strict=strict, name="ts", small=is_small(out))
        return S.add(eng, lambda e: e.tensor_scalar(out=out, in0=in0, scalar1=s1, scalar2=s2, op0=op0, op1=op1),
                     reads=reads, writes=writes, strict=strict, name="ts", small=is_small(out))

    def stt(out, in0, scalar, in1, op0, op1, reads, writes, strict=()):
        return S.add("dve", lambda e: e.scalar_tensor_tensor(out=out, in0=in0, scalar=scalar, in1=in1,
                                                             op0=op0, op1=op1),
                     reads=reads, writes=writes, strict=strict, name="stt", small=is_small(out))

    def cp(out, in_, reads, writes, eng="dve"):
        if eng == "act":
            return S.add(eng, lambda e: e.activation(out=out, in_=in_, func=AF.Copy), reads=reads, writes=writes,
                         name="cp", small=is_small(out))
        return S.add(eng, lambda e: e.tensor_copy(out=out, in_=in_), reads=reads, writes=writes, name="cp",
                     small=is_small(out))

    def memset(ap, val, writes, eng="dve", reads=()):
        return S.add(eng, lambda e: e.memset(ap, val), reads=reads, writes=writes, name="memset",
                     small=is_small(ap))

    def load_T(rows_ap, n, dest, dest_keys, stage, stage_key):
        dma_sp(stage[0:n, :], rows_ap, reads=BAR, writes=[stage_key])
        for j in range(4):
            b = psb()
            for q in range(4):
                k = 4 * j + q
                tr(PS[:, b, q * 128:q * 128 + n], stage[0:n, k * 128:(k + 1) * 128], IDF[0:n, 0:n],
                   reads=[stage_key, "idf"], writes=[("ps", b)])
            src = PS[:, b, :].rearrange("p (q c) -> p q c", q=4)[:, :, 0:n]
            cp(dest[:, 4 * j:4 * j + 4, :], src, reads=[("ps", b)],
               writes=[kk for q in range(4) for kk in dest_keys(4 * j + q)],
               eng=("dve" if j % 2 == 0 else "act"))

    def store_T(src, n, keys_of, outs, stage, stage_key):
        for j in range(4):
            b = psb()
            for q in range(4):
                k = 4 * j + q
                tr(PS[0:n, b, q * 128:(q + 1) * 128], src[:, k, :], IDF[:, :],
                   reads=list(keys_of(k)) + ["idf"], writes=[("ps", b)])
            cp(stage[0:n, j * 512:(j + 1) * 512], PS[0:n, b, :], reads=[("ps", b)] + BAR, writes=[(stage_key, j)],
               eng=("dve" if j % 2 == 0 else "act"))
        for (dap, r0, r1) in outs:
            dma_sp(dap, stage[r0:r1, :], reads=[(stage_key, j) for j in range(4)], is_out=True)

    def rmsnorm(gname, RSTD, SQ, out_fn, keep=False):
        assert not st["held"]
        st["bank"] = 0
        banks = [psb(True) for _ in NTILES]
        assert banks == [0, 1, 2]
        for k in range(KC):
            sq = SQ[k % 2]
            act(sq[:, 0:T], X[:, k, 0:T], AF.Square, reads=kx(k, 0, T) + BAR, writes=[("sq", k % 2)])
            for t, (a, b_) in enumerate(NTILES):
                mm(PS[:, banks[t], 0:b_ - a], ONEB[:, :], sq[:, a:b_], k == 0, k == KC - 1,
                   reads=[("sq", k % 2), "oneb"], writes=[("ps", banks[t])])
        for t, (a, b_) in enumerate(NTILES):
            act(PS[:, banks[t], 0:b_ - a], PS[:, banks[t], 0:b_ - a], AF.Ln, reads=[("ps", banks[t])] + BAR,
                writes=[("ps", banks[t])], scale=1.0 / D, bias=EPSB[:, 0:1])
            act(PS[:, banks[t], 0:b_ - a], PS[:, banks[t], 0:b_ - a], AF.Exp, reads=[("ps", banks[t])],
                writes=[("ps", banks[t])], scale=-0.5)
        for k in range(KC):
            out_fn(k)
        if not keep:
            release(*banks)

    RS_PS = PS[:, 0:3, :].rearrange("p b c -> p (b c)")
    RSK = [("ps", 0), ("ps", 1), ("ps", 2)]

    def norm_to_H(gname, RSTD):
        def f(k):
            stt(H[:, k, 0:T], X[:, k, 0:T], vcol(gname, k), RS_PS[:, 0:T], ALU.mult, ALU.mult,
                reads=kx(k, 0, T) + RSK, writes=kh(k, 0, T))
        return f

    def proj_out_partial(w2d, r0, nkf, rhs_fn, rhs_keys, cols_tiles, evac):
        assert nkf * 1024 <= SLOT_ELEMS
        for half in range(2):
            wp, wk = load_w(wview(w2d, r0, nkf, half * 1024, 1024))
            for mo8 in range(8):
                mo = half * 8 + mo8
                banks = [psb() for _ in cols_tiles]
                for kf in range(nkf):
                    for t, (a, b_) in enumerate(cols_tiles):
                        mm(PS[:, banks[t], 0:b_ - a], wp[:, kf, mo8 * 128:(mo8 + 1) * 128], rhs_fn(kf, a, b_),
                           kf == 0, kf == nkf - 1, reads=[wk] + rhs_keys(kf, a, b_), writes=[("ps", banks[t])])
                for t, (a, b_) in enumerate(cols_tiles):
                    evac(mo, banks[t], a, b_)

    def x_add_evac(mo, b, a, b_):
        tt(X[:, mo, a:b_], PS[:, b, 0:b_ - a], X[:, mo, a:b_], ALU.add,
           reads=[("ps", b)], writes=kx(mo, a, b_))

    EPSB = sb("EPSB", [128, 1], F32)
    DUMMY = sb("DUMMY", [128, 2], F32)
    memset(EPSB[:, :], EPS, writes=["epsb"])
    S.barrier()
    dma_sp(VEC[:, :], vecs_d[:, :], writes=["vec"])
    dma_sp(META[:, :], meta_d[:, :], writes=["meta"])
    dma_sp(IDF[:, :], ident_d[:, :], writes=["idf"])
    memset(ONEB[:, :], 1.0, writes=["oneb"])
    memset(ONEF[:, :], 1.0, writes=["onef"])
    memset(GHB[:, :, :], 0.0, writes=[("gh", k) for k in range(KC)])
    cp(IDB[:, :], IDF[:, :], reads=["idf"], writes=["idb"])
    scr_reset()
    WSN = scr_f32(1024).rearrange("p (h j) -> p h j", h=8)
    BDF = scr_f32(512)
    BDF3 = BDF[0:64, :].rearrange("p (h i) -> p h i", h=8)
    STG = [scr_f32(2048), scr_f32(2048)]
    blocks = [(r, 128) for r in range(0, 1152, 128)] + [(1152, 80)]
    for bi, (r0, n) in enumerate(blocks):
        load_T(xin[r0:r0 + n, :], n, X[:, :, r0:r0 + n], lambda k, r0=r0, n=n: kx(k, r0, r0 + n),
               STG[bi % 2], ("stg", bi % 2))

    dma_sp(BSB[:, :, :], gmlp_b_s.partition_broadcast(128), writes=["bsb"])
    dma_sp(WSN, gmlp_w_s.rearrange("h i j -> i h j"), reads=BAR, writes=["wsn"])
    memset(WSN[0:64, :, 64:128], 0.0, writes=["wsn"])
    memset(BDF[:, :], 0.0, writes=["bdf"], eng="pool")
    for s in range(4):
        for hd in range(8):
            dma_sp(BDF3[16 * s:16 * s + 16, hd, 16 * s:16 * s + 16],
                   gmlp_w_s[hd, 0:16, 0:16].rearrange("i j -> j i"), reads=BAR, writes=["bdf"], slow=True)
    cp(BDB[:, :, :], BDF3, reads=["bdf"], writes=["bdb"], eng="pool")
    for j in range(2):
        b = psb()
        for q in range(4):
            hd = 4 * j + q
            tr(PS[:, b, q * 128:(q + 1) * 128], WSN[:, hd, :], IDF[:, :], reads=["wsn", "idf"], writes=[("ps", b)])
        cp(WST[:, 4 * j:4 * j + 4, :], PS[:, b, :].rearrange("p (q c) -> p q c", q=4), reads=[("ps", b)],
           writes=["wst"])
    def ffn(l, tiles=TILES):
        cb = tiles[0][0]
        S.barrier()
        scr_reset()
        RSTD = scr_f32(T)
        SQ = [scr_bf16(T), scr_bf16(T)]
        HFF = [scr_bf16(4 * 1216).rearrange("p (k t) -> p k t", k=4) for _ in range(2)]
        GSB = [scr_f32(1216) for _ in range(2)]
        rmsnorm("gffn%d" % l, RSTD, SQ, norm_to_H("gffn%d" % l, RSTD))
        wg, wu, wd = ffn_w_gate[l], ffn_w_up[l], ffn_w_down[l]
        nslab = DFF // 512
        gi = [0]

        def down(sl):
            hb = HFF[sl % 2]
            proj_out_partial(wd, sl * 512, 4,
                             lambda kf, a, b_: hb[:, kf, a - cb:b_ - cb],
                             lambda kf, a, b_: [("hff", sl % 2, kf)],
                             tiles, x_add_evac)

        for sl in range(nslab):
            hb = HFF[sl % 2]
            for half in range(2):
                wgp, wgk = load_w(wview(wg, 0, KC, sl * 512 + half * 256, 256))
                wup, wuk = load_w(wview(wu, 0, KC, sl * 512 + half * 256, 256))
                for mi in range(2):
                    m = half * 2 + mi
                    gs = GSB[gi[0] % 2]
                    gkey = ("gsb", gi[0] % 2)
                    gi[0] += 1
                    banks = [psb() for _ in tiles]
                    for k in range(KC):
                        for t, (a, b_) in enumerate(tiles):
                            mm(PS[:, banks[t], 0:b_ - a], wgp[:, k, mi * 128:(mi + 1) * 128], H[:, k, a:b_],
                               k == 0, k == KC - 1, reads=[wgk] + kh(k, a, b_), writes=[("ps", banks[t])])
                    for t, (a, b_) in enumerate(tiles):
                        act(gs[:, a - cb:b_ - cb], PS[:, banks[t], 0:b_ - a], AF.Silu,
                            reads=[("ps", banks[t])] + BAR, writes=[gkey])
                    banks2 = [psb() for _ in tiles]
                    for k in range(KC):
                        for t, (a, b_) in enumerate(tiles):
                            mm(PS[:, banks2[t], 0:b_ - a], wup[:, k, mi * 128:(mi + 1) * 128], H[:, k, a:b_],
                               k == 0, k == KC - 1, reads=[wuk] + kh(k, a, b_), writes=[("ps", banks2[t])])
                    for t, (a, b_) in enumerate(tiles):
                        tt(hb[:, m, a - cb:b_ - cb], PS[:, banks2[t], 0:b_ - a], gs[:, a - cb:b_ - cb], ALU.mult,
                           reads=[("ps", banks2[t]), gkey] + BAR, writes=[("hff", sl % 2, m)])
            if sl >= 1:
                down(sl - 1)
        down(nslab - 1)

    def pool_mixer():
        S.barrier()
        scr_reset()
        W = TP + 124
        SQ = [scr_bf16(T), scr_bf16(T)]
        STGp = scr_f32(2048)
        HIST = scr_f32(KC * 60).rearrange("p (k c) -> p k c", k=KC)
        DIFF = scr_bf16(4 * 1216).rearrange("p (k t) -> p k t", k=4)
        PST = scr_f32(KC * 75).rearrange("p (k c) -> p k c", k=KC)
        HPs = [scr_bf16(4 * W).rearrange("p (k c) -> p k c", k=4)] * 2
        OST = STGp
        rmsnorm("gmix0", None, SQ, lambda k: None, keep=True)
        load_T(st_pool[:, :], 60, HIST, lambda k: [("hist", k)], STGp, "stgp")

        def sv(ap2d):
            return ap2d.rearrange("p (s c) -> p s c", s=4)

        ptiles = [(16, 528), (528, 1040), (1040, TP)]
        for g in range(4):
            w = 2 << g
            HP = HPs[g % 2]

            def v3(c4, o, n):
                return HP[:, c4, TP:TP + 124].rearrange("p (s c) -> p s c", s=4)[:, :, o:o + n]

            for c4 in range(4):
                c = 4 * g + c4
                hk = ("hp", c4)
                stt(HP[:, c4, 0:TP], X[:, c, 0:TP], vcol("gmix0", c), RS_PS[:, 0:TP], ALU.mult, ALU.mult,
                    reads=kx(c, 0, TP) + RSK + BAR, writes=[hk])
                stt(v3(c4, 15, 16), sv(X[:, c, TP:T]), vcol("gmix0", c), sv(RS_PS[:, TP:T]), ALU.mult, ALU.mult,
                    reads=kx(c, TP, T) + RSK + BAR, writes=[hk])
                cp(v3(c4, 0, 15), HIST[:, c, :].rearrange("p (s c) -> p s c", s=4), reads=[("hist", c)] + BAR,
                   writes=[hk], eng="act")
                stt(PST[:, c, 0:15], X[:, c, TP - 15:TP], vcol("gmix0", c), RS_PS[:, TP - 15:TP], ALU.mult, ALU.mult,
                    reads=kx(c, 0, TP) + RSK + BAR, writes=[("pst", c)])
                stt(PST[:, c, 15:75].rearrange("p (s c) -> p s c", s=4), sv(X[:, c, TP:T])[:, :, 1:16],
                    vcol("gmix0", c), sv(RS_PS[:, TP:T])[:, :, 1:16], ALU.mult, ALU.mult,
                    reads=kx(c, TP, T) + RSK + BAR, writes=[("pst", c)])
                banks = [psb() for _ in ptiles]
                for i_ in range(w):
                    for t, (a, b_) in enumerate(ptiles):
                        mm(PS[:, banks[t], 0:b_ - a], IDB[:, :], HP[:, c4, a - i_:b_ - i_], i_ == 0, i_ == w - 1,
                           reads=[hk, "idb"], writes=[("ps", banks[t])])
                    mm(PS[:, banks[2], 128:192], IDB[:, :], v3(c4, 15 - i_, 16), False, i_ == w - 1,
                       reads=[hk, "idb"], writes=[("ps", banks[2])], skip=True)
                tt(PS[:, banks[0], MAIN0 - 16:MAIN0], PS[:, banks[0], MAIN0 - 16:MAIN0], META[:, g * 16:(g + 1) * 16],
                   ALU.mult, reads=[("ps", banks[0]), "meta"], writes=[("ps", banks[0])])
                for t, (a, b_) in enumerate(ptiles):
                    stt(DIFF[:, c4, a - C0:b_ - C0], PS[:, banks[t], 0:b_ - a], 1.0 / w, HP[:, c4, a:b_],
                        ALU.mult, ALU.subtract, reads=[("ps", banks[t]), hk] + BAR, writes=[("diff", c4)])
                stt(sv(DIFF[:, c4, TP - C0:T - C0]), sv(PS[:, banks[2], 128:192]), 1.0 / w, v3(c4, 15, 16),
                    ALU.mult, ALU.subtract, reads=[("ps", banks[2]), hk] + BAR, writes=[("diff", c4)])
            wp, wk = load_w(pool_w[g].rearrange("(k p) n -> p k n", p=128))
            for mo in range(4):
                banks = [psb() for _ in TILES]
                for kc in range(4):
                    for t, (a, b_) in enumerate(TILES):
                        mm(PS[:, banks[t], 0:b_ - a], wp[:, kc, mo * 128:(mo + 1) * 128],
                           DIFF[:, kc, a - C0:b_ - C0], kc == 0, kc == 3,
                           reads=[wk, ("diff", kc)], writes=[("ps", banks[t])])
                for t, (a, b_) in enumerate(TILES):
                    ch = 4 * g + mo
                    stt(X[:, ch, a:b_], PS[:, banks[t], 0:b_ - a], vcol("pscale", ch), X[:, ch, a:b_],
                        ALU.mult, ALU.add, reads=[("ps", banks[t]), "vec"], writes=kx(ch, a, b_))
        release(0, 1, 2)
        store_T(PST, 75, lambda k: [("pst", k)], [(o_pool_p[:, :], 0, 15), (o_pool_s[:, :], 15, 75)], OST, "ost")

    def ln_finish(G, bsum, bsq, MEAN, RS, MSQ, msqkey="msq"):
        act(MEAN[:, 0:G], PS[:, bsum, 0:G], AF.Copy, reads=[("ps", bsum)] + BAR, writes=["mean"], scale=1.0 / D)
        tt(MSQ[:, 0:G], MEAN[:, 0:G], MEAN[:, 0:G], ALU.mult, reads=["mean"] + BAR, writes=[msqkey])
        stt(PS[:, bsq, 0:G], PS[:, bsq, 0:G], 1.0 / D, MSQ[:, 0:G], ALU.mult, ALU.subtract,
            reads=[("ps", bsq), msqkey] + BAR, writes=[("ps", bsq)])
        act(PS[:, bsq, 0:G], PS[:, bsq, 0:G], AF.Ln, reads=[("ps", bsq)], writes=[("ps", bsq)], bias=EPSB[:, 0:1],
            strict=["epsb"])
        act(PS[:, bsq, 0:G], PS[:, bsq, 0:G], AF.Exp, reads=[("ps", bsq)], writes=[("ps", bsq)], scale=-0.5)
        stt(PS[:, bsum, 0:G], MEAN[:, 0:G], -1.0, PS[:, bsq, 0:G], ALU.mult, ALU.mult,
            reads=["mean", ("ps", bsq)] + BAR, writes=[("ps", bsum)])

    def gmlp_mixer():
        S.barrier()
        scr_reset()
        RSTD = scr_f32(T)
        SQ = [scr_bf16(T), scr_bf16(T)]
        rmsnorm("gmix1", RSTD, SQ, norm_to_H("gmix1", RSTD))
        GM = 384
        for gi_, (c0, c1) in enumerate(GROUPS):
            G = c1 - c0
            is_s = (gi_ == 3)
            S.barrier()
            scr_reset()
            BIG = scr_f32(KC * G).rearrange("p (k c) -> p k c", k=KC)
            VB = scr_bf16(2048)
            CSQ = [scr_f32(GM), scr_f32(GM)]
            MEAN = scr_f32(GM)
            RS = scr_f32(GM)
            MSQ = CSQ[0]
            VNB = [scr_bf16(GM), scr_bf16(GM)]
            UT = [scr_f32(GM), scr_f32(GM)]
            GS = [scr_bf16(4 * GM).rearrange("p (k c) -> p k c", k=4) for _ in range(2)]
            if is_s:
                VST = scr_f32(KC * 64).rearrange("p (k c) -> p k c", k=KC)
                OSTv = scr_f32(2048)
            bsum, bsq = psb(True), psb(True)
            pend = None

            def stats(f, first, last):
                cs = CSQ[f % 2]
                act(cs[:, 0:G], BIG[:, f, 0:G], AF.Square, reads=[("big", f)] + BAR, writes=[("csq", f % 2)])
                mm(PS[:, bsum, 0:G], ONEF[:, :], BIG[:, f, 0:G], first, last,
                   reads=[("big", f), "onef"], writes=[("ps", bsum)])
                mm(PS[:, bsq, 0:G], ONEF[:, :], cs[:, 0:G], first, last,
                   reads=[("csq", f % 2), "onef"], writes=[("ps", bsq)])

            for f in range(KC):
                if f % 2 == 0:
                    wp, wk = load_w(wview(gmlp_w_in, 0, KC, D + f * 128, 256))
                b = psb()
                for k in range(KC):
                    mm(PS[:, b, 0:G], wp[:, k, (f % 2) * 128:(f % 2 + 1) * 128], H[:, k, c0:c1], k == 0, k == KC - 1,
                       reads=[wk] + kh(k, c0, c1), writes=[("ps", b)])
                act(BIG[:, f, 0:G], PS[:, b, 0:G], AF.Gelu_apprx_tanh, reads=[("ps", b), "vec"] + BAR,
                    writes=[("big", f)], bias=vcol("bv", f))
                if pend is not None:
                    stats(pend, pend == 0, False)
                pend = f
            stats(pend, False, True)
            ln_finish(G, bsum, bsq, MEAN, RS, MSQ, msqkey=("csq", 0))
            ntile = (G + 127) // 128
            tw = [min(128, G - j * 128) for j in range(ntile)]
            PSB = PS[:, :, :].bitcast(BF16)
            tb = [[psb(True), psb(True)] for _ in range(ntile)]
            for f in range(KC):
                stt(BIG[:, f, 0:G], BIG[:, f, 0:G], vcol("lng", f), PS[:, bsq, 0:G], ALU.mult, ALU.mult,
                    reads=[("big", f), ("ps", bsq), "vec"] + BAR, writes=[("big", f)])
                stt(BIG[:, f, 0:G], PS[:, bsum, 0:G], vcol("lng", f), BIG[:, f, 0:G], ALU.mult, ALU.add,
                    reads=[("big", f), ("ps", bsum), "vec"] + BAR, writes=[("big", f)])
                vn = VNB[f % 2]
                act(vn[:, 0:G], BIG[:, f, 0:G], AF.Identity, reads=[("big", f), "vec"] + BAR,
                    writes=[("vnb", f % 2)], bias=vcol("lnb", f))
                if is_s:
                    act(VST[:, f, 0:64], BIG[:, f, 0:G], AF.Identity, reads=[("big", f), "vec"] + BAR,
                        writes=[("vst", f)], bias=vcol("lnb", f))
                for j in range(ntile):
                    bb = tb[j][f // 8]
                    tr(PSB[0:tw[j], bb, (f % 8) * 128:(f % 8 + 1) * 128], vn[:, j * 128:j * 128 + tw[j]], IDB[:, :],
                       reads=[("vnb", f % 2), "idb"], writes=[("ps", bb)])
            release(bsum, bsq)
            for j in range(ntile):
                n = tw[j]
                for hh in range(2):
                    cp(VB[0:n, hh * 1024:(hh + 1) * 1024], PSB[0:n, tb[j][hh], :], reads=[("ps", tb[j][hh])] + BAR,
                       writes=[("vb", hh)], eng=("dve" if hh == 0 else "act"))
                release(tb[j][0], tb[j][1])
                for q in range(4):
                    b = psb()
                    for fi in range(4):
                        f = 4 * q + fi
                        hd = f // 2
                        rhs = BDB[0:n, hd, 0:n] if is_s else WST[0:n, hd, 0:n]
                        mm(PS[:, b, fi * 128:fi * 128 + n], VB[0:n, f * 128:(f + 1) * 128], rhs, True, True,
                           reads=[("vb", f // 8), "bdb", "wst"], writes=[("ps", b)])
                    for hp in range(2):
                        hd = 2 * q + hp
                        for fi in range(2):
                            f = 4 * q + 2 * hp + fi
                            src = PS[:, b, (2 * hp + fi) * 128:(2 * hp + fi) * 128 + n]
                            if is_s:
                                for s_ in range(4):
                                    tt(BIG[:, f, 16 * s_:16 * s_ + 16], src[:, 16 * s_:16 * s_ + 16],
                                       BSB[:, hd, 0:16], ALU.add,
                                       reads=[("ps", b), "bsb"] + BAR, writes=[("big", f)])
                            else:
                                tt(BIG[:, f, j * 128:j * 128 + n], src, BSB[:, hd, 0:n], ALU.add,
                                   reads=[("ps", b), "bsb"] + BAR, writes=[("big", f)])
            u0 = max(c0, TILES_H[0][0])
            uo = u0 - c0
            Gu = c1 - u0
            for f in range(KC):
                if f % 2 == 0:
                    wp, wk = load_w(wview(gmlp_w_in, 0, KC, f * 128, 256))
                b = psb()
                for k in range(KC):
                    mm(PS[:, b, 0:Gu], wp[:, k, (f % 2) * 128:(f % 2 + 1) * 128], H[:, k, u0:c1], k == 0, k == KC - 1,
                       reads=[wk] + kh(k, u0, c1), writes=[("ps", b)])
                ut = UT[f % 2]
                act(ut[:, 0:Gu], PS[:, b, 0:Gu], AF.Gelu_apprx_tanh, reads=[("ps", b), "vec"] + BAR,
                    writes=[("ut", f % 2)], bias=vcol("bu", f))
                sl = f // 4
                gsb_ = GS[sl % 2]
                tt(gsb_[:, f % 4, 0:Gu], ut[:, 0:Gu], BIG[:, f, uo:G], ALU.mult,
                   reads=[("ut", f % 2), ("big", f)] + BAR, writes=[("gs", sl % 2, f % 4)])
                if f % 4 == 3:
                    proj_out_partial(gmlp_w_out, sl * 512, 4,
                                     lambda kf, a, b_, gsb_=gsb_: gsb_[:, kf, a - u0:b_ - u0],
                                     lambda kf, a, b_, sl=sl: [("gs", sl % 2, kf)],
                                     [(u0, c1)], x_add_evac)
            if is_s:
                store_T(VST, 64, lambda k: [("vst", k)], [(o_v[:, :], 0, 64)], OSTv, "ostv")

    def sconv_mixer(tiles=TILES_H):
        cb = tiles[0][0]
        S.barrier()
        scr_reset()
        SH2 = scr_f32(KC * 8).rearrange("p (k c) -> p k c", k=KC)
        keep = st["scr"]
        RSTD = scr_f32(T)
        SQ = [scr_bf16(T), scr_bf16(T)]
        STG2 = scr_f32(2048)
        rmsnorm("gmix2", RSTD, SQ, norm_to_H("gmix2", RSTD))
        load_T(st_sconv[:, :], 8, SH2, lambda k: [("sh2", k)], STG2, "stg2")
        S.barrier()
        st["scr"] = keep
        NP = TP - cb
        CS = scr_f32(T - cb)
        CXP = scr_f32(2 + NP)
        CXS = scr_f32(4 * 18).rearrange("p (s c) -> p s c", s=4)
        CVP = scr_f32(NP)
        CVS = scr_f32(64).rearrange("p (s c) -> p s c", s=4)
        SST = scr_f32(KC * 10).rearrange("p (k c) -> p k c", k=KC)
        Z = [scr_bf16(4 * 1216).rearrange("p (k t) -> p k t", k=4) for _ in range(2)]
        OST = scr_f32(2048)
        memset(CXP[:, 0:2], 0.0, writes=["cxp"], reads=BAR)
        wb = wc = wx = None
        for m in range(KC):
            if m % 2 == 0:
                wc = load_w(wview(sconv_w_in, 0, KC, D + m * 128, 256))
                wx = load_w(wview(sconv_w_in, 0, KC, 2 * D + m * 128, 256))
                wb = load_w(wview(sconv_w_in, 0, KC, m * 128, 256))
            mi = m % 2

            def proj(wpk):
                wp, wk = wpk
                banks = [psb() for _ in tiles]
                for k in range(KC):
                    for t, (a, b_) in enumerate(tiles):
                        mm(PS[:, banks[t], 0:b_ - a], wp[:, k, mi * 128:(mi + 1) * 128], H[:, k, a:b_],
                           k == 0, k == KC - 1, reads=[wk] + kh(k, a, b_), writes=[("ps", banks[t])])
                return banks

            bc = proj(wc)
            for t, (a, b_) in enumerate(tiles):
                cp(CS[:, a - cb:b_ - cb], PS[:, bc[t], 0:b_ - a], reads=[("ps", bc[t])] + BAR, writes=["cs"], eng="act")
            bx = proj(wx)
            for t, (a, b_) in enumerate(tiles):
                pe_ = min(b_, TP)
                tt(CXP[:, 2 + a - cb:2 + pe_ - cb], PS[:, bx[t], 0:pe_ - a], CS[:, a - cb:pe_ - cb], ALU.mult,
                   reads=[("ps", bx[t]), "cs"] + BAR, writes=["cxp"])
                if b_ > TP:
                    tt(CXS[:, :, 2:18], PS[:, bx[t], TP - a:T - a].rearrange("p (s c) -> p s c", s=4),
                       CS[:, TP - cb:T - cb].rearrange("p (s c) -> p s c", s=4), ALU.mult,
                       reads=[("ps", bx[t]), "cs"] + BAR, writes=["cxs"])
            cp(CXS[:, :, 0:2], SH2[:, m, :].rearrange("p (s c) -> p s c", s=4), reads=[("sh2", m)] + BAR,
               writes=["cxs"])
            ts(CXP[:, 2 + MAIN0 - cb - 2:2 + MAIN0 - cb], CXP[:, 2 + MAIN0 - cb - 2:2 + MAIN0 - cb], META[:, 64:65],
               None, ALU.mult, None, reads=["cxp"], writes=["cxp"], strict=["meta"])
            cp(SST[:, m, 0:2], CXP[:, 2 + NP - 2:2 + NP], reads=["cxp"] + BAR, writes=[("sst", m)], eng="act")
            cp(SST[:, m, 2:10].rearrange("p (s c) -> p s c", s=4), CXS[:, :, 16:18], reads=["cxs"] + BAR,
               writes=[("sst", m)], eng="act")
            ts(CVP[:, 0:NP], CXP[:, 0:NP], vcol("scw0", m), None, ALU.mult, None, reads=["cxp", "vec"] + BAR,
               writes=["cvp"])
            stt(CVP[:, 0:NP], CXP[:, 1:1 + NP], vcol("scw1", m), CVP[:, 0:NP], ALU.mult, ALU.add,
                reads=["cxp", "cvp"], writes=["cvp"])
            stt(CVP[:, 0:NP], CXP[:, 2:2 + NP], vcol("scw2", m), CVP[:, 0:NP], ALU.mult, ALU.add,
                reads=["cxp", "cvp"], writes=["cvp"])
            ts(CVS[:, :, :], CXS[:, :, 0:16], vcol("scw0", m), None, ALU.mult, None, reads=["cxs", "vec"] + BAR,
               writes=["cvs"])
            stt(CVS[:, :, :], CXS[:, :, 1:17], vcol("scw1", m), CVS[:, :, :], ALU.mult, ALU.add,
                reads=["cxs", "cvs"], writes=["cvs"])
            stt(CVS[:, :, :], CXS[:, :, 2:18], vcol("scw2", m), CVS[:, :, :], ALU.mult, ALU.add,
                reads=["cxs", "cvs"], writes=["cvs"])
            bb = proj(wb)
            sl = m // 4
            zb = Z[sl % 2]
            for t, (a, b_) in enumerate(tiles):
                pe_ = min(b_, TP)
                tt(zb[:, m % 4, a - cb:pe_ - cb], PS[:, bb[t], 0:pe_ - a], CVP[:, a - cb:pe_ - cb], ALU.mult,
                   reads=[("ps", bb[t]), "cvp"] + BAR, writes=[("z", sl % 2, m % 4)])
                if b_ > TP:
                    tt(zb[:, m % 4, TP - cb:T - cb].rearrange("p (s c) -> p s c", s=4),
                       PS[:, bb[t], TP - a:T - a].rearrange("p (s c) -> p s c", s=4), CVS[:, :, :], ALU.mult,
                       reads=[("ps", bb[t]), "cvs"] + BAR, writes=[("z", sl % 2, m % 4)])
            if m % 4 == 3:
                proj_out_partial(sconv_w_out, sl * 512, 4,
                                 lambda kf, a, b_, zb=zb: zb[:, kf, a - cb:b_ - cb],
                                 lambda kf, a, b_, sl=sl: [("z", sl % 2, kf)],
                                 tiles, x_add_evac)
        store_T(SST, 10, lambda k: [("sst", k)], [(o_sconv_p[:, :], 0, 2), (o_sconv_s[:, :], 2, 10)], OST, "ost2")

    def cconv_mixer():
        S.barrier()
        scr_reset()
        RSTD = scr_f32(T)
        SQ = [scr_bf16(T), scr_bf16(T)]
        rmsnorm("gmix3", RSTD, SQ, norm_to_H("gmix3", RSTD))
        dma_sp(o_cconv_s.rearrange("(s r) d -> s r d", r=30)[:, 0:14, :],
               st_cconv.rearrange("(s r) d -> s r d", r=30)[:, 16:30, :], is_out=True)
        GM = 384
        ND = 8
        for gi_, (c0, c1) in enumerate(GROUPS3):
            G = c1 - c0
            is_s = (gi_ == 3)
            S.barrier()
            scr_reset()
            BIG = scr_f32(KC * G).rearrange("p (k c) -> p k c", k=KC)
            CST = scr_f32(KC * 94).rearrange("p (k c) -> p k c", k=KC)
            SG2 = [scr_f32(GM), scr_f32(GM)]
            CSQ = [scr_f32(GM), scr_f32(GM)]
            MEAN = scr_f32(GM)
            RS = None
            MSQ = CSQ[0]
            DG = [scr_bf16(128) for _ in range(ND)]
            if is_s:
                GHS = scr_f32(KC * 120).rearrange("p (k c) -> p k c", k=KC)
                STG3 = scr_f32(2048)
                GLUB3s = [scr_bf16(4 * 46).rearrange("p (s c) -> p s c", s=4) for _ in range(2)]
                load_T(st_cconv[:, :], 120, GHS, lambda k: [("ghs", k)], STG3, "stg3")
            else:
                GLUBs = [scr_bf16(32 + GM) for _ in range(2)]
            bsum, bsq = psb(True), psb(True)
            pend = None
            dgi = [0]

            def stats(f, first, last):
                cs = CSQ[f % 2]
                act(cs[:, 0:G], BIG[:, f, 0:G], AF.Square, reads=[("big", f)] + BAR, writes=[("csq", f % 2)])
                mm(PS[:, bsum, 0:G], ONEF[:, :], BIG[:, f, 0:G], first, last,
                   reads=[("big", f), "onef"], writes=[("ps", bsum)])
                mm(PS[:, bsq, 0:G], ONEF[:, :], cs[:, 0:G], first, last,
                   reads=[("csq", f % 2), "onef"], writes=[("ps", bsq)])

            wst = {}

            def pw1_part(m):
                if m % 2 == 0:
                    wst["g"] = load_w(wview(cconv_w_pw1, 0, KC, D + m * 128, 256))
                    wst["a"] = load_w(wview(cconv_w_pw1, 0, KC, m * 128, 256))
                wgp, wgk = wst["g"]
                wap, wak = wst["a"]
                mi = m % 2
                SGm = SG2[m % 2]
                sgk = ("sg", m % 2)
                gk = ("glu", m % 2)
                bg_ = psb()
                for k in range(KC):
                    mm(PS[:, bg_, 0:G], wgp[:, k, mi * 128:(mi + 1) * 128], H[:, k, c0:c1], k == 0, k == KC - 1,
                       reads=[wgk] + kh(k, c0, c1), writes=[("ps", bg_)])
                act(SGm[:, 0:G], PS[:, bg_, 0:G], AF.Sigmoid, reads=[("ps", bg_), "vec"] + BAR, writes=[sgk],
                    bias=vcol("bg", m))
                ba_ = psb()
                for k in range(KC):
                    mm(PS[:, ba_, 0:G], wap[:, k, mi * 128:(mi + 1) * 128], H[:, k, c0:c1], k == 0, k == KC - 1,
                       reads=[wak] + kh(k, c0, c1), writes=[("ps", ba_)])
                if is_s:
                    GL3 = GLUB3s[m % 2]
                    cp(GL3[:, :, 0:30], GHS[:, m, :].rearrange("p (s c) -> p s c", s=4), reads=[("ghs", m)] + BAR,
                       writes=[gk], eng="act")
                    stt(GL3[:, :, 30:46], PS[:, ba_, 0:64].rearrange("p (s c) -> p s c", s=4), vcol("ba", m),
                        SGm[:, 0:64].rearrange("p (s c) -> p s c", s=4), ALU.add, ALU.mult,
                        reads=[("ps", ba_), sgk, "vec"] + BAR, writes=[gk])
                    stt(CST[:, m, 30:94], PS[:, ba_, 0:64], vcol("ba", m), SGm[:, 0:64], ALU.add, ALU.mult,
                        reads=[("ps", ba_), sgk, "vec"] + BAR, writes=[("cst", m)])
                else:
                    GL = GLUBs[m % 2]
                    cp(GL[:, 0:30], GHB[:, m, :], reads=[("gh", m)] + BAR, writes=[gk], eng="act")
                    stt(GL[:, 30:30 + G], PS[:, ba_, 0:G], vcol("ba", m), SGm[:, 0:G], ALU.add, ALU.mult,
                        reads=[("ps", ba_), sgk, "vec"] + BAR, writes=[gk])
                    if gi_ == 2:
                        stt(CST[:, m, 0:30], PS[:, ba_, G - 30:G], vcol("ba", m), SGm[:, G - 30:G], ALU.add, ALU.mult,
                            reads=[("ps", ba_), sgk, "vec"] + BAR, writes=[("cst", m)])
                    if gi_ == 0:
                        lo = 30 + (MAIN0 - 30 - c0)
                        ts(GL[:, lo:lo + 30], GL[:, lo:lo + 30], META[:, 64:65], None, ALU.mult, None,
                           reads=[gk], writes=[gk], strict=["meta"])
                    cp(GHB[:, m, :], GL[:, G:G + 30], reads=[gk], writes=[("gh", m)], eng="act")

            def conv_part(m):
                gk = ("glu", m % 2)
                bc_ = psb()
                for tap in range(31):
                    di = dgi[0] % ND
                    dgi[0] += 1
                    dg = DG[di]
                    if tap % 3 == 2:
                        S.add("act", lambda e, dg=dg, tap=tap, m=m: e.activation(
                            out=dg[:, 0:128], in_=IDF[:, :], func=AF.Copy, scale=vcol("dww%d" % tap, m)),
                            reads=["idf", "vec"] + BAR, writes=[("dg", di)], name="diag")
                    else:
                        S.add("dve", lambda e, dg=dg, tap=tap, m=m: e.tensor_scalar(
                            out=dg[:, 0:128], in0=IDF[:, :], scalar1=vcol("dww%d" % tap, m), scalar2=None,
                            op0=ALU.mult),
                            reads=["idf", "vec"] + BAR, writes=[("dg", di)], name="diag")
                    if is_s:
                        rhs = GLUB3s[m % 2][:, :, tap:tap + 16]
                    else:
                        rhs = GLUBs[m % 2][:, tap:tap + G]
                    mm(PS[:, bc_, 0:G], dg[:, 0:128], rhs, tap == 0, tap == 30,
                       reads=[("dg", di), gk], writes=[("ps", bc_)])
                act(BIG[:, m, 0:G], PS[:, bc_, 0:G], AF.Identity, reads=[("ps", bc_), "vec"] + BAR,
                    writes=[("big", m)], bias=vcol("dwb", m))

            for m in range(KC + 1):
                if m < KC:
                    pw1_part(m)
                if m >= 1:
                    conv_part(m - 1)
                    if pend is not None:
                        stats(pend, pend == 0, False)
                    pend = m - 1
            stats(pend, False, True)
            ln_finish(G, bsum, bsq, MEAN, RS, MSQ, msqkey=("csq", 0))
            for m in range(KC):
                tt(BIG[:, m, 0:G], BIG[:, m, 0:G], PS[:, bsq, 0:G], ALU.mult, reads=[("big", m), ("ps", bsq)] + BAR,
                   writes=[("big", m)])
                tt(BIG[:, m, 0:G], BIG[:, m, 0:G], PS[:, bsum, 0:G], ALU.add, reads=[("big", m), ("ps", bsum)] + BAR,
                   writes=[("big", m)])
                act(H[:, m, c0:c1], BIG[:, m, 0:G], AF.Silu, reads=[("big", m), "vec"] + BAR, writes=kh(m, c0, c1),
                    scale=vcol("clg", m), bias=vcol("clb", m))
            release(bsum, bsq)
            if gi_ == 2:
                S.barrier()
                st["scr"] = 0
                OSTa = scr_f32(2048)
                assert KC * G >= 2048
                store_T(CST[:, :, 0:30], 30, lambda k: [("cst", k)], [(o_cconv_p[:, :], 0, 30)], OSTa, "ost3a")
            if is_s:
                S.barrier()
                OSTb = STG3
                store_T(CST[:, :, 30:94], 64, lambda k: [("cst", k)],
                        [(o_cconv_s.rearrange("(s r) d -> s r d", r=30)[s_, 14:30, :], 16 * s_, 16 * s_ + 16)
                         for s_ in range(4)], OSTb, "ost3b")
        tiles = TILES_M
        for mo in range(KC):
            if mo % 2 == 0:
                wp, wk = load_w(wview(cconv_w_pw2, 0, KC, mo * 128, 256))
            banks = [psb() for _ in tiles]
            for k in range(KC):
                for t, (a, b_) in enumerate(tiles):
                    mm(PS[:, banks[t], 0:b_ - a], wp[:, k, (mo % 2) * 128:(mo % 2 + 1) * 128], H[:, k, a:b_],
                       k == 0, k == KC - 1, reads=[wk] + kh(k, a, b_), writes=[("ps", banks[t])])
            for t, (a, b_) in enumerate(tiles):
                stt(X[:, mo, a:b_], PS[:, banks[t], 0:b_ - a], vcol("b2", mo), X[:, mo, a:b_], ALU.add, ALU.add,
                    reads=[("ps", banks[t]), "vec"], writes=kx(mo, a, b_))

    def final_out():
        S.barrier()
        scr_reset()
        RSTD = scr_f32(T)
        SQ = [scr_bf16(T), scr_bf16(T)]
        OSTs = [scr_f32(2048), scr_f32(2048)]

        def f(k):
            stt(X[:, k, MAIN0:T], X[:, k, MAIN0:T], vcol("gfin", k), RS_PS[:, MAIN0:T], ALU.mult, ALU.mult,
                reads=kx(k, MAIN0, T) + RSK, writes=kx(k, MAIN0, T))
        rmsnorm("gfin", RSTD, SQ, f)
        oblocks = [(MAIN0 + 128 * i, 128) for i in range(8)] + [(TP, 64)]
        for bi, (cb, n) in enumerate(oblocks):
            store_T(X[:, :, cb:cb + n], n, lambda k, cb=cb, n=n: kx(k, cb, cb + n),
                    [(o_y[cb - MAIN0:cb - MAIN0 + n, :], 0, n)], OSTs[bi % 2], ("osty", bi % 2))

    stages = [("m0", pool_mixer), ("f0", lambda: ffn(0)), ("m1", gmlp_mixer), ("f1", lambda: ffn(1, TILES_H)),
              ("m2", sconv_mixer), ("f2", lambda: ffn(2, TILES_H)), ("m3", cconv_mixer),
              ("f3", lambda: ffn(3, TILES_M))]
    stopped = False
    for name_, fn_ in stages:
        fn_()
        if debug_stop == name_:
            dbg = nc.dram_tensor("dbg_x", [128, KC * T], F32, kind="ExternalOutput").ap()
            dbgh = nc.dram_tensor("dbg_h", [128, KC * T], BF16, kind="ExternalOutput").ap()
            S.barrier()
            dma_sp(dbg[:, :], X[:, :, :].rearrange("p k t -> p (k t)"), reads=BAR, is_out=True)
            dma_sp(dbgh[:, :], H[:, :, :].rearrange("p k t -> p (k t)"), reads=BAR, is_out=True)
            stopped = True
            break
    if not stopped:
        final_out()

    sems = {}
    sem_ctx = []
    for e in Sched.ENGS:
        cm_ = nc.semaphore("sem_" + e)
        sems[e] = cm_.__enter__()
        sem_ctx.append(cm_)
    dsem = []
    for i in range(S.nch):
        cm_ = nc.semaphore("dsem%d" % i)
        dsem.append(cm_.__enter__())
        sem_ctx.append(cm_)
    for e in Sched.ENGS:
        for op in S.ops[e]:
            for d in op.deps:
                if d.dma_ch is None:
                    d.sig = True
    cnt = {e: 0 for e in Sched.ENGS}
    for e in Sched.ENGS:
        for op in S.ops[e]:
            if op.sig:
                cnt[e] += 1
                op.sig = cnt[e]

    def emit(e, eng):
        waited = {}
        for op in S.ops[e]:
            for d in op.deps:
                if d.dma_ch is not None:
                    key, val, sem = ("d", d.dma_ch), 16 * d.dma_cnt, dsem[d.dma_ch]
                else:
                    key, val, sem = ("e", d.eng), d.sig, sems[d.eng]
                if waited.get(key, 0) >= val:
                    continue
                waited[key] = val
                eng.wait_ge(sem, val)
            if op.fn is None:
                ins = eng.memset(DUMMY[:, 0:1], 0.0)
            else:
                ins = op.fn(eng)
            if op.dma_ch is not None:
                ins.then_inc(dsem[op.dma_ch], 16)
            elif op.sig:
                ins.then_inc(sems[e], 1)
        if e == "sp":
            for ch in range(S.nch):
                if S.dma_cnt.get(ch, 0):
                    eng.wait_ge(dsem[ch], 16 * S.dma_cnt[ch])

    with nc.Block() as block:
        @block.tensor
        def _(eng):
            emit("pe", eng)

        @block.scalar
        def _(eng):
            emit("act", eng)

        @block.vector
        def _(eng):
            emit("dve", eng)

        @block.gpsimd
        def _(eng):
            emit("pool", eng)

        @block.sync
        def _(eng):
            emit("sp", eng)

    for cm_ in reversed(sem_ctx):
        cm_.__exit__(None, None, None)
    for cm_ in reversed(ctx):
        cm_.__exit__(None, None, None)
    stats = {e: len(S.ops[e]) for e in Sched.ENGS}
    return nc, stats


def _pack_vecs(inp):
    def col(v):
        return np.ascontiguousarray(np.asarray(v, np.float32).reshape(16, 128).T)
    parts = {}
    for i in range(4):
        parts["gmix%d" % i] = col(inp["norm_mix_g"][i])
        parts["gffn%d" % i] = col(inp["norm_ffn_g"][i])
    parts["gfin"] = col(inp["norm_final_g"])
    parts["pscale"] = col(inp["pool_scale"])
    parts["bu"] = col(inp["gmlp_b_in"][:D])
    parts["bv"] = col(inp["gmlp_b_in"][D:])
    parts["lng"] = col(inp["gmlp_ln_g"])
    parts["lnb"] = col(inp["gmlp_ln_b"])
    for i in range(3):
        parts["scw%d" % i] = col(inp["sconv_conv_w"][i])
    parts["ba"] = col(inp["cconv_b_pw1"][:D])
    parts["bg"] = col(inp["cconv_b_pw1"][D:])
    for i in range(31):
        parts["dww%d" % i] = col(inp["cconv_dw_w"][i])
    parts["dwb"] = col(inp["cconv_dw_b"])
    parts["clg"] = col(inp["cconv_ln_g"])
    parts["clb"] = col(inp["cconv_ln_b"])
    parts["b2"] = col(inp["cconv_b_pw2"])
    return np.ascontiguousarray(np.concatenate([parts[n] for n in VEC_NAMES], axis=1))


_CACHE = {}


def make_in_maps(inp):
    vecs = _pack_vecs(inp)
    ident = np.eye(128, dtype=np.float32)
    xp = inp["x_prompt"]
    xs = inp["x_sample"]
    shared = {
        "vecs": vecs, "ident": ident,
        "pool_w": inp["pool_w"], "gmlp_w_in": inp["gmlp_w_in"], "gmlp_w_s": inp["gmlp_w_s"],
        "gmlp_b_s": inp["gmlp_b_s"], "gmlp_w_out": inp["gmlp_w_out"], "sconv_w_in": inp["sconv_w_in"],
        "sconv_w_out": inp["sconv_w_out"], "cconv_w_pw1": inp["cconv_w_pw1"], "cconv_w_pw2": inp["cconv_w_pw2"],
        "ffn_w_gate": inp["ffn_w_gate"], "ffn_w_up": inp["ffn_w_up"], "ffn_w_down": inp["ffn_w_down"],
    }
    in_maps = []
    for c in range(NCORE):
        b, q = c // 4, c % 4
        p0 = q * 1024
        xin = np.zeros((T, D), np.float32)
        lo = p0 - (PRE + HALO)
        if lo >= 0:
            xin[0:TP] = xp[b, lo:lo + TP]
        else:
            xin[-lo:TP] = xp[b, 0:lo + TP]
        xin[TP:T] = xs[4 * c:4 * c + 4].reshape(64, D)
        meta = np.ones((128, 65), np.float32)
        if q == 0:
            for g, w in enumerate((2, 4, 8, 16)):
                for j in range(16):
                    meta[:, g * 16 + j] = float(w) / float(min(j + 1, w))
            meta[:, 64] = 0.0
        m = dict(shared)
        m["xin"] = xin
        m["st_pool"] = np.ascontiguousarray(inp["state_pool"][4 * c:4 * c + 4].reshape(60, D))
        m["st_sconv"] = np.ascontiguousarray(inp["state_sconv"][4 * c:4 * c + 4].reshape(8, D))
        m["st_cconv"] = np.ascontiguousarray(inp["state_cconv"][4 * c:4 * c + 4].reshape(120, D))
        m["meta"] = meta
        in_maps.append(m)
    return in_maps


def kernel(**inp):
    inp = {k: np.asarray(v) for k, v in inp.items()}
    if "nc" not in _CACHE:
        _CACHE["nc"] = build_program()
    nc, _ = _CACHE["nc"]
    in_maps = make_in_maps(inp)
    res = run_bass_kernel_spmd(nc, in_maps, core_ids=list(range(NCORE)))
    R = res.results
    y_prompt = np.zeros((2, 4096, D), np.float32)
    y_sample = np.zeros((32, 16, D), np.float32)
    pool_p = np.zeros((2, 15, D), np.float32)
    pool_s = np.zeros((32, 15, D), np.float32)
    v_s = np.zeros((32, 16, D), np.float32)
    sconv_p = np.zeros((2, 2, D), np.float32)
    sconv_s = np.zeros((32, 2, D), np.float32)
    cconv_p = np.zeros((2, 30, D), np.float32)
    cconv_s = np.zeros((32, 30, D), np.float32)
    for c in range(NCORE):
        b, q = c // 4, c % 4
        r = R[c]
        y_prompt[b, q * 1024:(q + 1) * 1024] = r["o_y"][0:1024]
        y_sample[4 * c:4 * c + 4] = r["o_y"][1024:1088].reshape(4, 16, D)
        pool_s[4 * c:4 * c + 4] = r["o_pool_s"].reshape(4, 15, D)
        v_s[4 * c:4 * c + 4] = r["o_v"].reshape(4, 16, D)
        sconv_s[4 * c:4 * c + 4] = r["o_sconv_s"].reshape(4, 2, D)
        cconv_s[4 * c:4 * c + 4] = r["o_cconv_s"].reshape(4, 30, D)
        if q == 3:
            pool_p[b] = r["o_pool_p"]
            sconv_p[b] = r["o_sconv_p"]
            cconv_p[b] = r["o_cconv_p"]
    return (y_prompt, y_sample, pool_p, pool_s, v_s, sconv_p, sconv_s, cconv_p, cconv_s)
```
